# Optimizing a Trainium2 kernel written in Bass

```python
import math
import jax, jax.numpy as jnp
from jax import lax
import numpy as np

D_MODEL = 1024
BATCH = 16
SEQ = 2048
DEPTH = 4

CHUNK = 64
A_LEFT_CHUNKS = 8
A_BAND = (A_LEFT_CHUNKS + 1) * CHUNK
HEAD_DIM = 64
MIX_WIDTH = D_MODEL
A_HEADS = 8
A_WIDTH = A_HEADS * HEAD_DIM
B_HEADS = 4
B_WIDTH = B_HEADS * 2 * HEAD_DIM
A_MAX_REL = 128
A_REL_SIZE = 2 * A_MAX_REL + 1
T5_BUCKETS = 32
T5_MAX_DIST = 128
D_FF = 2816
CONV_WIDTH = 3
Q_BLOCK = 128
RMS_EPS = 1e-6
IN_COLS = 3 * A_WIDTH + 3 * B_WIDTH

kernel_name = "hybrid_chunked_diff_convffn_trunk"


def rmsnorm(x, g):
    xf = x.astype(jnp.float32)
    y = xf * lax.rsqrt(jnp.mean(xf * xf, axis=-1, keepdims=True) + RMS_EPS)
    return (y * g.astype(jnp.float32)).astype(x.dtype)


def t5_bucket(rel):
    nb = T5_BUCKETS // 2
    ret = jnp.where(rel > 0, nb, 0)
    n = jnp.abs(rel)
    max_exact = nb // 2
    is_small = n < max_exact
    nf = jnp.maximum(n, 1).astype(jnp.float32)
    large = max_exact + (jnp.log(nf / max_exact) / math.log(T5_MAX_DIST / max_exact)
                         * (nb - max_exact)).astype(jnp.int32)
    large = jnp.minimum(large, nb - 1)
    return ret + jnp.where(is_small, n, large)


def chunked_band_attention(q, k, v, rel_table):
    b, s, h, dh = q.shape
    nc = s // CHUNK
    qc = q.reshape(b, nc, CHUNK, h, dh)
    pad = ((0, 0), (A_LEFT_CHUNKS * CHUNK, 0), (0, 0), (0, 0))
    kp = jnp.pad(k, pad).reshape(b, nc + A_LEFT_CHUNKS, CHUNK, h, dh)
    vp = jnp.pad(v, pad).reshape(b, nc + A_LEFT_CHUNKS, CHUNK, h, dh)
    kb = jnp.concatenate([kp[:, j:j + nc] for j in range(A_LEFT_CHUNKS + 1)], axis=2)
    vb = jnp.concatenate([vp[:, j:j + nc] for j in range(A_LEFT_CHUNKS + 1)], axis=2)
    scores = jnp.einsum("bcqhd,bckhd->bhcqk", qc, kb).astype(jnp.float32) * (dh ** -0.5)
    qq = jnp.arange(CHUNK)[:, None]
    m = jnp.arange(A_BAND)[None, :]
    dist = A_LEFT_CHUNKS * CHUNK + qq - m
    idx = jnp.clip(dist, -A_MAX_REL, A_MAX_REL) + A_MAX_REL
    bias = rel_table[:, idx].astype(jnp.float32)
    kpos = (jnp.arange(nc)[:, None] - A_LEFT_CHUNKS) * CHUNK + jnp.arange(A_BAND)[None, :]
    valid = (kpos >= 0)[None, None, :, None, :]
    scores = jnp.where(valid, scores + bias[:, None], -jnp.inf)
    p = jax.nn.softmax(scores, axis=-1).astype(v.dtype)
    o = jnp.einsum("bhcqk,bckhd->bcqhd", p, vb)
    return o.reshape(b, s, h * dh)


def diff_attention(q, k, v, t5_table, lam, subln_g, lam_init):
    b, s = q.shape[0], q.shape[1]
    nblk = s // Q_BLOCK
    qblocks = q.reshape(b, nblk, Q_BLOCK, 2 * B_HEADS, HEAD_DIM).transpose(1, 0, 2, 3, 4)
    kpos = jnp.arange(s)
    kchunk = kpos // CHUNK

    def block(args):
        qblk, i = args
        qpos = i * Q_BLOCK + jnp.arange(Q_BLOCK)
        sc = jnp.einsum("bqhd,bkhd->bhqk", qblk, k).astype(jnp.float32) * (HEAD_DIM ** -0.5)
        bucket = t5_bucket(kpos[None, :] - qpos[:, None])
        bias = jnp.transpose(t5_table[bucket], (2, 0, 1)).astype(jnp.float32)
        allowed = kchunk[None, :] <= (qpos // CHUNK)[:, None]
        sc = jnp.where(allowed, sc + bias, -jnp.inf)
        p = jax.nn.softmax(sc, axis=-1).reshape(b, B_HEADS, 2, Q_BLOCK, s)
        w = p[:, :, 0] - lam * p[:, :, 1]
        return jnp.einsum("bhqk,bkhe->bqhe", w.astype(v.dtype), v)

    o = lax.map(block, (qblocks, jnp.arange(nblk)))
    o = o.transpose(1, 0, 2, 3, 4).reshape(b, s, B_HEADS, 2 * HEAD_DIM)
    o = rmsnorm(o, subln_g) * (1.0 - lam_init)
    return o.reshape(b, s, B_WIDTH)


def causal_dwconv(u, w, bias):
    s = u.shape[1]
    up = jnp.pad(u, ((0, 0), (CONV_WIDTH - 1, 0), (0, 0)))
    out = bias + up[:, 0:s] * w[0]
    for j in range(1, CONV_WIDTH):
        out = out + up[:, j:j + s] * w[j]
    return out


def setup_inputs(seed: int = 0) -> dict:
    key = jax.random.key(seed)
    ks = jax.random.split(key, 20)
    f32 = jnp.float32
    nrm = lambda k, shape, scale: (jax.random.normal(k, shape, f32) * scale)
    return {
        "x": nrm(ks[0], (BATCH, SEQ, D_MODEL), 1.0),
        "attn_norm_g": 1.0 + nrm(ks[1], (DEPTH, D_MODEL), 0.02),
        "w_in": nrm(ks[2], (DEPTH, D_MODEL, IN_COLS), D_MODEL ** -0.5),
        "a_rel_bias": nrm(ks[3], (DEPTH, A_HEADS, A_REL_SIZE), 0.2),
        "t5_bias": nrm(ks[4], (T5_BUCKETS, 2 * B_HEADS), 0.2),
        "lambda_q1": nrm(ks[5], (DEPTH, HEAD_DIM), 0.1),
        "lambda_k1": nrm(ks[6], (DEPTH, HEAD_DIM), 0.1),
        "lambda_q2": nrm(ks[7], (DEPTH, HEAD_DIM), 0.1),
        "lambda_k2": nrm(ks[8], (DEPTH, HEAD_DIM), 0.1),
        "subln_g": 1.0 + nrm(ks[9], (DEPTH, 2 * HEAD_DIM), 0.02),
        "w_out": nrm(ks[10], (DEPTH, MIX_WIDTH, D_MODEL), MIX_WIDTH ** -0.5),
        "ffn_norm_g": 1.0 + nrm(ks[11], (DEPTH, D_MODEL), 0.02),
        "w_up": nrm(ks[12], (DEPTH, D_MODEL, 2 * D_FF), D_MODEL ** -0.5),
        "conv_w": nrm(ks[13], (DEPTH, CONV_WIDTH, 2 * D_FF), CONV_WIDTH ** -0.5),
        "conv_b": nrm(ks[14], (DEPTH, 2 * D_FF), 0.02),
        "w_down": nrm(ks[15], (DEPTH, D_FF, D_MODEL), D_FF ** -0.5),
        "final_norm_g": 1.0 + nrm(ks[16], (D_MODEL,), 0.02),
    }


def reference(x, attn_norm_g, w_in, a_rel_bias, t5_bias, lambda_q1, lambda_k1, lambda_q2,
              lambda_k2, subln_g, w_out, ffn_norm_g, w_up, conv_w, conv_b, w_down, final_norm_g):
    b, s, _ = x.shape
    for l in range(DEPTH):
        h = rmsnorm(x, attn_norm_g[l])
        proj = h @ w_in[l]
        qa, ka, va, qb, kb, vb = jnp.split(proj, 6, axis=-1)
        oa = chunked_band_attention(qa.reshape(b, s, A_HEADS, HEAD_DIM),
                                    ka.reshape(b, s, A_HEADS, HEAD_DIM),
                                    va.reshape(b, s, A_HEADS, HEAD_DIM),
                                    a_rel_bias[l])
        lam_init = 0.8 - 0.6 * math.exp(-0.3 * l)
        lam = (jnp.exp(jnp.sum(lambda_q1[l].astype(jnp.float32) * lambda_k1[l].astype(jnp.float32)))
               - jnp.exp(jnp.sum(lambda_q2[l].astype(jnp.float32) * lambda_k2[l].astype(jnp.float32)))
               + lam_init)
        ob = diff_attention(qb.reshape(b, s, 2 * B_HEADS, HEAD_DIM),
                            kb.reshape(b, s, 2 * B_HEADS, HEAD_DIM),
                            vb.reshape(b, s, B_HEADS, 2 * HEAD_DIM),
                            t5_bias, lam, subln_g[l], lam_init)
        x = x + jnp.concatenate([oa, ob], axis=-1) @ w_out[l]
        h = rmsnorm(x, ffn_norm_g[l])
        u = causal_dwconv(h @ w_up[l], conv_w[l], conv_b[l])
        gate, val = jnp.split(u, 2, axis=-1)
        x = x + (jax.nn.silu(gate) * val) @ w_down[l]
    return rmsnorm(x, final_norm_g)
```

```python
import math
from contextlib import ExitStack

import numpy as np
import concourse.bass as bass
import concourse.mybir as mybir
from concourse.bass_utils import run_bass_kernel_spmd

F32 = mybir.dt.float32
BF16 = mybir.dt.bfloat16
AF = mybir.ActivationFunctionType
ALU = mybir.AluOpType
AX = mybir.AxisListType

D = 1024
S_LEN = 2048
DEPTH = 4
DFF = 2816
NJ = DFF // 128
NEG = -30000.0
RMS_EPS = 1e-6
N_CORES = 8
SEQ_PER_CORE = 2
NS_W = 3

ENGS = ("pe", "act", "dve", "pool", "sp")


class Op:
    __slots__ = ("eng", "fn", "waits", "signal", "sem", "val", "is_dma", "group")

    def __init__(self, eng, fn, is_dma=False):
        self.eng = eng
        self.fn = fn
        self.waits = []
        self.signal = False
        self.sem = None
        self.val = None
        self.is_dma = is_dma
        self.group = None


class Sched:
    def __init__(self, nc, stack):
        self.nc = nc
        self.stack = stack
        self.ops = {e: [] for e in ENGS}
        self.res = {}
        self.esem = {e: stack.enter_context(nc.semaphore("tl_" + e)) for e in ENGS}
        self.dma_cum = {}
        self.dry = False
        self.const_reads = []
        self.cur_group = {e: None for e in ENGS}
        self.ngroups = 0

    def dma_sem(self, name):
        s = self.stack.enter_context(self.nc.semaphore(name))
        self.dma_cum[s] = 0
        return s

    def add(self, eng, fn, reads=(), writes=(), dma_sem=None):
        if self.dry:
            return None
        op = Op(eng, fn, is_dma=dma_sem is not None)
        deps = {}
        if self.const_reads:
            reads = list(reads) + self.const_reads

        def dep(o, raw):
            if o is None or o is op:
                return
            if not o.is_dma and not op.is_dma and o.eng == eng and eng == "pe":
                return
            deps[id(o)] = o

        res = self.res
        for r in reads:
            st = res.get(r)
            if st is not None:
                dep(st[0], True)
        for w in writes:
            st = res.get(w)
            if st is not None:
                dep(st[0], True)
                for o in st[1].values():
                    dep(o, False)
                for o in st[2]:
                    dep(o, False)
        for r in reads:
            st = res.get(r)
            if st is None:
                st = res[r] = [None, {}, []]
            if op.is_dma:
                st[2].append(op)
            else:
                st[1][eng] = op
        for w in writes:
            res[w] = [op, {}, []]
        for o in deps.values():
            o.signal = True
            op.waits.append(o)
        if dma_sem is not None:
            self.dma_cum[dma_sem] += 16
            op.sem = dma_sem
            op.val = self.dma_cum[dma_sem]
            op.signal = True
        op.group = self.cur_group[eng]
        self.ops[eng].append(op)
        return op

    def begin_group(self, eng):
        self.ngroups += 1
        self.cur_group[eng] = self.ngroups

    def end_group(self, eng):
        self.cur_group[eng] = None

    def finalize(self):
        for e in ENGS:
            n = 0
            for op in self.ops[e]:
                if op.is_dma:
                    continue
                if op.signal:
                    n += 1
                    op.sem = self.esem[e]
                    op.val = n

    def emit(self, e, eng):
        waited = {}
        ops = self.ops[e]
        for idx, op in enumerate(ops):
            need = {}
            members = [op]
            if op.group is not None and (idx == 0 or ops[idx - 1].group != op.group):
                j = idx + 1
                while j < len(ops) and ops[j].group == op.group:
                    members.append(ops[j])
                    j += 1
            for mop in members:
                for o in mop.waits:
                    if o.group is not None and o.group == op.group and o.eng == e:
                        continue
                    k = o.sem
                    if waited.get(k, 0) >= o.val:
                        continue
                    if need.get(k, 0) < o.val:
                        need[k] = o.val
            for k, v in need.items():
                eng.wait_ge(k, v)
                waited[k] = v
            ins = op.fn(eng)
            if op.signal:
                ins.then_inc(op.sem, 16 if op.is_dma else 1)

    def emit_all(self, tail_waits=()):
        nc = self.nc
        self.finalize()
        with nc.Block() as block:
            @block.tensor
            def _(eng):
                self.emit("pe", eng)

            @block.scalar
            def _(eng):
                self.emit("act", eng)

            @block.vector
            def _(eng):
                self.emit("dve", eng)

            @block.gpsimd
            def _(eng):
                self.emit("pool", eng)

            @block.sync
            def _(eng):
                self.emit("sp", eng)
                for (s, v) in tail_waits:
                    eng.wait_ge(s, v)


class Stream:
    def __init__(self, S, name, nslots, issue_fn):
        self.S = S
        self.name = name
        self.n = nslots
        self.issue_fn = issue_fn
        self.descs = []
        self.next_issue = 0
        self.next_get = 0
        self.sems = [S.dma_sem("%s_s%d" % (name, i)) for i in range(nslots)]

    def reset(self):
        self.next_issue = 0
        self.next_get = 0

    def _issue(self, i):
        slot = i % self.n
        for fn in self.issue_fn(slot, self.descs[i]):
            self.S.add("pool", fn, writes=[(self.name, slot)], dma_sem=self.sems[slot])

    def get(self, desc):
        i = self.next_get
        self.next_get += 1
        if self.S.dry:
            self.descs.append(desc)
            return i % self.n
        assert self.descs[i] == desc, (self.name, i, self.descs[i], desc)
        upto = min(len(self.descs), i + self.n - 1)
        while self.next_issue < upto:
            self._issue(self.next_issue)
            self.next_issue += 1
        return i % self.n


def build_nc(layers=(0, 1, 2, 3), final_norm=True, nseq=SEQ_PER_CORE):
    nc = bass.Bass("TRN2", target_bir_lowering=False)
    dt = lambda name, shape, kind="ExternalInput": nc.dram_tensor(name, list(shape), F32, kind=kind).ap()
    x_d = dt("x", [nseq, S_LEN, D])
    w_in_d = dt("w_in", [DEPTH, D, 3072])
    w_out_d = dt("w_out", [DEPTH, D, D])
    w_up_d = dt("w_up", [DEPTH, D, 2 * DFF])
    w_down_d = dt("w_down", [DEPTH, DFF, D])
    ta_d = dt("ta", [DEPTH, 8, 128, 640])
    tb_d = dt("tb", [8, 128, 640])
    g1_d = dt("g1", [128, DEPTH * 8])
    g2_d = dt("g2", [128, DEPTH * 8])
    gf_d = dt("gf", [128, 8])
    cw_d = dt("cw", [128, DEPTH * 3 * 44])
    cb_d = dt("cb", [128, DEPTH * 44])
    sg_d = dt("sg", [128, DEPTH])
    lam_d = dt("lam", [128, 4 * DEPTH * 64])
    t5f_d = dt("t5f", [128, 8])
    out_d = dt("out", [nseq, S_LEN, D], kind="ExternalOutput")

    with ExitStack() as st:
        S = Sched(nc, st)
        sb = lambda n, s, d: st.enter_context(nc.sbuf_tensor(n, list(s), d))
        xT = sb("xT", [128, 8, S_LEN], F32)
        hT = sb("hT", [128, 8, S_LEN], BF16)
        U = sb("U", [128, 22528], BF16)
        ct = sb("ct", [128, 2, 1024], F32)
        PT = sb("PT", [128, 8, 512], BF16)
        TAB = sb("TAB", [128, 4, 640], BF16)
        wr = sb("wr", [128, NS_W, 4096], BF16)
        sq = sb("sq", [128, 2, 512], BF16)
        lnv = sb("lnv", [128, 512], F32)
        rstd = sb("rstd", [128, 512], F32)
        scr = sb("scr", [128, 5, 512], F32)
        r0 = scr[:, 0, :]
        r1 = scr[:, 1, :]
        av = scr[:, 2, :]
        bv = scr[:, 3, :]
        ov = scr[:, 4, :]
        scrf = scr[:, :, :].rearrange("p a t -> p (a t)")
        lamr = scrf[:, 0:4 * DEPTH * 64]
        lamp = scrf[:, 4 * DEPTH * 64:6 * DEPTH * 64]
        carry = sb("carry", [128, 44, 2], F32)
        identb = sb("identb", [128, 128], BF16)
        identf = sb("identf", [128, 128], F32)
        onesb = sb("onesb", [128, 128], BF16)
        eps_t = sb("eps_t", [128, 1], F32)
        g1 = sb("g1s", [128, DEPTH * 8], F32)
        g2 = sb("g2s", [128, DEPTH * 8], F32)
        gf = sb("gfs", [128, 8], F32)
        cw = sb("cws", [128, DEPTH * 3 * 44], F32)
        cb = sb("cbs", [128, DEPTH * 44], F32)
        sg = sb("sgs", [128, DEPTH], F32)
        lams = sb("lams", [128, 2 * DEPTH], F32)
        nlam = sb("nlam", [128, DEPTH], F32)
        t5f = sb("t5fs", [128, 8], F32)
        ps = st.enter_context(nc.psum_tensor("ps", [128, 8, 512], F32))

        OT = U[:, 0:8192].rearrange("p (c t) -> p c t", c=4)
        Vb = U[:, 8192:16384].rearrange("p (i n) -> p i n", i=16)
        QT0 = U[:, 16384:18432]
        KT = U[:, 18432:20480]
        QT1 = U[:, 20480:22528]
        QTS = (QT0, QT1)
        Gb = U[:, 0:22528].rearrange("p (j t) -> p j t", j=NJ)

        def psr(b):
            return [("ps", b, 0), ("ps", b, 1)]

        def w_issue(slot, desc):
            fns = []
            kc, ntot, parts = desc[0], desc[1], desc[2]
            dst = wr[:, slot, 0:kc * ntot].rearrange("p (c n) -> p c n", c=kc)
            for (which, l, r0_, c0, n, off) in parts:
                src_t = {"in": w_in_d, "out": w_out_d, "up": w_up_d, "down": w_down_d}[which]
                src = src_t[l, r0_:r0_ + kc * 128, c0:c0 + n].rearrange("(c p) n -> p c n", p=128)
                fns.append((lambda d_, s_: (lambda e: e.dma_start(out=d_, in_=s_)))(dst[:, :, off:off + n], src))
            return fns

        def tab_issue(slot, desc):
            kind, l, h = desc
            src = ta_d[l, h] if kind == "a" else tb_d[h]
            return [(lambda d_, s_: (lambda e: e.dma_start(out=d_, in_=s_)))(TAB[:, slot, :], src)]

        WS = Stream(S, "w", NS_W, w_issue)
        TS = Stream(S, "tab", 4, tab_issue)

        def wtile(kc, ntot, parts):
            slot = WS.get((kc, ntot, tuple(parts)))
            view = wr[:, slot, 0:kc * ntot].rearrange("p (c n) -> p c n", c=kc)
            return slot, view

        def mm(out, lhsT, rhs, start, stop, reads, writes):
            S.add("pe", lambda e: e.matmul(out, lhsT, rhs, start=start, stop=stop), reads=reads, writes=writes)

        def tr(out, in_, reads, writes):
            S.add("pe", lambda e: e.transpose(out, in_, identf[:]), reads=reads, writes=writes)

        def act(out, in_, func, reads, writes, **kw):
            S.add("act", lambda e: e.activation(out, in_, func, **kw), reads=reads, writes=writes)

        def tt(out, a, b, op, reads, writes, eng="dve"):
            S.add(eng, lambda e: e.tensor_tensor(out, a, b, op), reads=reads, writes=writes)

        def stt(out, in0, scalar, in1, op0, op1, reads, writes, eng="dve"):
            S.add(eng, lambda e: e.scalar_tensor_tensor(out, in0, scalar, in1, op0, op1), reads=reads, writes=writes)

        def ts1(out, in0, scalar, op, reads, writes):
            S.add("dve", lambda e: e.tensor_scalar(out, in0, scalar, None, op), reads=reads, writes=writes)

        def cp(out, in_, reads, writes, eng="dve"):
            S.add(eng, lambda e: e.tensor_copy(out, in_), reads=reads, writes=writes)

        def recip(out, in_, reads, writes):
            S.add("dve", lambda e: e.reciprocal(out, in_), reads=reads, writes=writes)

        def dma(eng, out, in_, reads, writes, sem):
            return S.add(eng, lambda e: e.dma_start(out=out, in_=in_), reads=reads, writes=writes, dma_sem=sem)

        def prologue():
            dc = S.dma_sem("dconst")
            cops = []
            for (dst, src, nm) in [(g1[:], g1_d, "g1"), (g2[:], g2_d, "g2"), (gf[:], gf_d, "gf"), (cw[:], cw_d, "cw"),
                                   (cb[:], cb_d, "cb"), (sg[:], sg_d, "sg"), (lamr, lam_d, "lamr"), (t5f[:], t5f_d, "t5f")]:
                cops.append(dma("sp", dst, src, [], [nm], dc))
            for o in cops:
                o.val = S.dma_cum[dc]
            S.add("pool", lambda e: e.memset(identf[:], 0.0), writes=["identf"])
            S.add("pool", lambda e: e.affine_select(out=identf[:], in_=identf[:], pattern=[[-1, 128]],
                                                    compare_op=ALU.not_equal, fill=1.0, base=0,
                                                    channel_multiplier=1),
                  reads=["identf"], writes=["identf"])
            S.add("pool", lambda e: e.memset(onesb[:], 1.0), writes=["onesb"])
            S.add("pool", lambda e: e.memset(eps_t[:], RMS_EPS), writes=["eps"])
            S.add("pool", lambda e: e.memset(carry[:], 0.0), writes=["carry"])
            cp(identb[:], identf[:], ["identf"], ["identb"])
            n64 = DEPTH * 64
            tt(lamp[:, 0:n64], lamr[:, 0:n64], lamr[:, n64:2 * n64], ALU.mult, ["lamr"], ["lamp0"])
            tt(lamp[:, n64:2 * n64], lamr[:, 2 * n64:3 * n64], lamr[:, 3 * n64:4 * n64], ALU.mult, ["lamr"], ["lamp1"])
            for i in range(2 * DEPTH):
                S.add("dve", (lambda o_, i_: (lambda e: e.tensor_reduce(o_, i_, AX.X, ALU.add)))(
                    lams[:, i:i + 1], lamp[:, i * 64:(i + 1) * 64]),
                      reads=["lamp0", "lamp1"], writes=[("lams", i)])
            act(lams[:], lams[:], AF.Exp, [("lams", i) for i in range(2 * DEPTH)], ["lamse"])
            tt(nlam[:], lams[:, DEPTH:2 * DEPTH], lams[:, 0:DEPTH], ALU.subtract, ["lamse"], ["nlam0"])
            for l in range(DEPTH):
                li = 0.8 - 0.6 * math.exp(-0.3 * l)
                ts1(nlam[:, l:l + 1], nlam[:, l:l + 1], -li, ALU.add, ["nlam0"], [("nlam", l)])
                ts1(sg[:, l:l + 1], sg[:, l:l + 1], 1.0 - li, ALU.mult, ["sg"], [("sg", l)])

        def rmsnorm_to(gain_tile, gcol0, dst_t, dst_name):
            for t4 in range(4):
                tsl = slice(t4 * 512, (t4 + 1) * 512)
                for c in range(8):
                    sl = c % 2
                    act(sq[:, sl, :], xT[:, c, tsl], AF.Square, [("xT", c, t4)], [("sq", sl)])
                    mm(ps[:, 7, :], onesb[:], sq[:, sl, :], c == 0, c == 7, [("sq", sl)], psr(7))
                act(lnv[:], ps[:, 7, :], AF.Ln, psr(7), [("lnv", 0), ("lnv", 1)], bias=eps_t[:, 0:1], scale=1.0 / D)
                act(rstd[:], lnv[:], AF.Exp, [("lnv", 0), ("lnv", 1)], ["rstd"], scale=-0.5)
                for c in range(8):
                    stt(dst_t[:, c, tsl], xT[:, c, tsl], gain_tile[:, gcol0 + c:gcol0 + c + 1], rstd[:],
                        ALU.mult, ALU.mult, [("xT", c, t4), "rstd"], [(dst_name, c, t4)])

        def load_x(s):
            for i in range(16):
                sl = i % 2
                dma("sp", ct[:, sl, :], x_d[s, i * 128:(i + 1) * 128, :], [], [("ct", sl)], xsem[sl])
                for half in range(2):
                    b = 2 * (i % 2) + half
                    for c4 in range(4):
                        c = half * 4 + c4
                        tr(ps[:, b, c4 * 128:(c4 + 1) * 128], ct[:, sl, c * 128:(c + 1) * 128], [("ct", sl)], psr(b))
                    dst = xT[:, half * 4:half * 4 + 4, i * 128:(i + 1) * 128]
                    src = ps[:, b, :].rearrange("p (c t) -> p c t", c=4)
                    wr_ = [("xT", half * 4 + c4, i // 4) for c4 in range(4)]
                    if half == 0:
                        act(dst, src, AF.Copy, psr(b), wr_)
                    else:
                        cp(dst, src, psr(b), wr_)

        def store_out(s):
            for i in range(16):
                sl = i % 2
                for half in range(2):
                    b = 2 * (i % 2) + half
                    for c4 in range(4):
                        c = half * 4 + c4
                        tr(ps[:, b, c4 * 128:(c4 + 1) * 128], xT[:, c, i * 128:(i + 1) * 128], [("xT", c, i // 4)], psr(b))
                    dst = ct[:, sl, half * 512:(half + 1) * 512]
                    if half == 0:
                        act(dst, ps[:, b, :], AF.Copy, psr(b) + [("ct", sl)], [("ct", sl, half)])
                    else:
                        cp(dst, ps[:, b, :], psr(b) + [("ct", sl)], [("ct", sl, half)])
                dma("sp", out_d[s, i * 128:(i + 1) * 128, :], ct[:, sl, :],
                    [("ct", sl, 0), ("ct", sl, 1)], [("ct", sl)], osem)

        def proj_v(l, c0):
            slot, W = wtile(8, 512, [("in", l, 0, c0, 512, 0)])
            for i in range(16):
                b = i % 3
                for k in range(8):
                    mm(ps[:, b, :], hT[:, k, i * 128:(i + 1) * 128], W[:, k, :], k == 0, k == 7,
                       [("hT", k, i // 4), ("w", slot)], psr(b))
                cp(Vb[:, i, :], ps[:, b, :], psr(b), [("V", i)])

        def proj_qk(l, cq, ck):
            slot, W = wtile(8, 256, [("in", l, 0, cq, 128, 0), ("in", l, 0, ck, 128, 128)])
            for which in range(2):
                for t4 in range(4):
                    b = (which * 4 + t4) % 3
                    for k in range(8):
                        mm(ps[:, b, :], W[:, k, which * 128:(which + 1) * 128], hT[:, k, t4 * 512:(t4 + 1) * 512],
                           k == 0, k == 7, [("hT", k, t4), ("w", slot)], psr(b))
                    tsl = slice(t4 * 512, (t4 + 1) * 512)
                    if which == 0:
                        ts1(QT0[0:64, tsl], ps[0:64, b, :], 0.125, ALU.mult, psr(b), [("QT", 0, t4)])
                        ts1(QT1[64:128, tsl], ps[64:128, b, :], 0.125, ALU.mult, psr(b), [("QT", 1, t4)])
                    else:
                        cp(KT[:, tsl], ps[:, b, :], psr(b), [("KT", t4)])

        st_rot = [0]
        pt_rot = [0]
        ST_BANKS = (0, 1, 2, 7)
        PIPE_DEPTH = 3

        def next_st():
            b = ST_BANKS[st_rot[0] % 4]
            st_rot[0] += 1
            return b

        def attn_steps_run(steps):
            n = len(steps)
            deferred = []
            for i in range(n + PIPE_DEPTH):
                S.begin_group("pe")
                if i < n:
                    steps[i][0]()
                if i >= PIPE_DEPTH:
                    steps[i - PIPE_DEPTH][1]()
                S.end_group("pe")
                for dq in deferred:
                    dq[0] -= 1
                while deferred and deferred[0][0] <= 0:
                    deferred.pop(0)[1]()
                if i >= PIPE_DEPTH and steps[i - PIPE_DEPTH][2] is not None:
                    later = steps[i - PIPE_DEPTH][2]()
                    if later is not None:
                        deferred.append([8, later])
            while deferred:
                deferred.pop(0)[1]()

        def score_step(kt_ap, qt_ap, c0, c1, tab_ap, tab_res, exp_bias, qk_reads, pv_list, post=None):
            def qk():
                b = next_st()
                pslot = pt_rot[0] % 8
                pt_rot[0] += 1
                st_[0] = (b, pslot)
                mm(ps[:, b, c0:c1], kt_ap, qt_ap, True, tab_ap is None, qk_reads, psr(b))
                if tab_ap is not None:
                    mm(ps[:, b, c0:c1], identb[:], tab_ap, False, True, [tab_res], psr(b))
                if exp_bias is None:
                    act(PT[:, pslot, c0:c1], ps[:, b, c0:c1], AF.Exp, psr(b), [("PT", pslot)])
                else:
                    act(PT[:, pslot, c0:c1], ps[:, b, c0:c1], AF.Exp, psr(b), [("PT", pslot)], bias=exp_bias)

            st_ = [None]

            def pv():
                b, pslot = st_[0]
                for (out_ap, lhsT_ap, start, stop, reads, writes) in pv_list:
                    mm(out_ap, lhsT_ap, PT[:, pslot, c0:c1], start, stop, [("PT", pslot)] + reads, writes)

            return (qk, pv, post)

        def recip_act(out, in_ps, in_res, out_res):
            act(lnv[:], in_ps, AF.Ln, in_res, ["lnv"])
            act(out, lnv[:], AF.Exp, ["lnv"], [out_res], scale=-1.0)

        def zero_q_pads():
            S.add("dve", lambda e: e.memset(QT0[64:128, :], 0.0), writes=[("G", 16), ("G", 17), ("QTz", 0)])
            S.add("dve", lambda e: e.memset(QT1[0:64, :], 0.0), writes=[("G", 20), ("G", 21), ("QTz", 1)])

        def attn_core(l, kind, unit, tslots, post_fns):
            steps = []
            for qt in range(4):
                for hh in range(2):
                    bO, bS = 3 + 2 * hh, 4 + 2 * hh
                    if kind == "a":
                        rs = [r for r in (4, 3, 5, 2, 6, 1, 7, 0) if 4 * qt - 4 + r >= 0]
                        blocks = []
                        for r in rs:
                            kb = 4 * qt - 4 + r
                            lo, hi = max(0, 2 * r - 8), min(7, 2 * r + 1)
                            c0, c1 = 64 * lo, 64 * (hi + 1)
                            v0 = 512 - 128 * r
                            blocks.append((kb, c0, c1, TAB[:, tslots[hh], v0 + c0:v0 + c1], ("tab", tslots[hh]), None))
                        vcol = slice(unit * 128, (unit + 1) * 128)
                    else:
                        m = 2 * unit + hh
                        blocks = []
                        kbs = list(range(4 * qt, 4 * qt + 4)) + ([4 * qt - 1] if qt > 0 else []) + list(range(0, max(0, 4 * qt - 1)))
                        for kb in kbs:
                            j = 4 * qt - kb
                            if j >= 2:
                                blocks.append((kb, 0, 512, None, None, t5f[:, m:m + 1]))
                            elif j == 1:
                                blocks.append((kb, 0, 512, TAB[:, tslots[hh], 128:640], ("tab", tslots[hh]), None))
                            else:
                                c0 = 128 * (-j)
                                blocks.append((kb, c0, 512, TAB[:, tslots[hh], 0:512 - c0], ("tab", tslots[hh]), None))
                        vcol = slice(unit * 128, (unit + 1) * 128)
                    nb = len(blocks)
                    for idx, (kb, c0, c1, tab_ap, tab_res, ebias) in enumerate(blocks):
                        first, last = idx == 0, idx == nb - 1
                        pv_list = [
                            (ps[:, bO, c0:c1], Vb[:, kb, vcol], first, last, [("V", kb)], psr(bO)),
                            (ps[:, bS, c0:c1], onesb[:], first, last, [], psr(bS)),
                        ]
                        post = None
                        if last:
                            post = (lambda f=post_fns[hh], qt_=qt: f(qt_))
                        steps.append(score_step(
                            KT[:, kb * 128:(kb + 1) * 128],
                            QTS[hh][:, qt * 512 + c0:qt * 512 + c1],
                            c0, c1, tab_ap, tab_res, ebias, [("KT", kb // 4), ("QT", hh, qt), ("QTz", hh)], pv_list,
                            post=post))
            attn_steps_run(steps)

        def attn_A(l):
            proj_v(l, 1024)
            for p in range(4):
                proj_qk(l, 128 * p, 512 + 128 * p)
                tslots = [TS.get(("a", l, 2 * p + hh)) for hh in range(2)]

                def mk_post(hh, p=p):
                    lo_, hi_ = 64 * hh, 64 * hh + 64
                    bO, bS = 3 + 2 * hh, 4 + 2 * hh
                    rr = r0 if hh == 0 else r1

                    def post(qt):
                        tsl = slice(qt * 512, (qt + 1) * 512)
                        recip(rr[lo_:hi_, :], ps[lo_:hi_, bS, :], psr(bS), [("rr", hh)])
                        tt(OT[lo_:hi_, p, tsl], ps[lo_:hi_, bO, :], rr[lo_:hi_, :], ALU.mult,
                           psr(bO) + [("rr", hh)], [("OT", p, qt, hh)])
                    return post

                attn_core(l, "a", p, tslots, [mk_post(0), mk_post(1)])
            w_out_round(l, 0)

        def attn_B(l):
            proj_v(l, 2560)
            for hb in range(4):
                proj_qk(l, 1536 + 128 * hb, 2048 + 128 * hb)
                tslots = [TS.get(("b", 0, 2 * hb + mmi)) for mmi in range(2)]

                def post0(qt):
                    recip(r0, ps[:, 4, :], psr(4), [("rr", 0)])
                    tt(av, ps[:, 3, :], r0, ALU.mult, psr(3) + [("rr", 0)], ["av"])

                def post1(qt, hb=hb):
                    tsl = slice(qt * 512, (qt + 1) * 512)
                    recip(r1, ps[:, 6, :], psr(6), [("rr", 1)])
                    stt(bv, ps[:, 5, :], nlam[:, l:l + 1], r1, ALU.mult, ALU.mult, psr(5) + [("rr", 1)], ["bv"])
                    tt(ov, av, bv, ALU.add, ["av", "bv"], ["ov"])
                    tt(sq[:, 0, :], ov, ov, ALU.mult, ["ov"], [("sq", 0)])

                    def later():
                        b7 = next_st()
                        mm(ps[:, b7, :], onesb[:], sq[:, 0, :], True, True, [("sq", 0)], psr(b7))
                        act(lnv[:], ps[:, b7, :], AF.Ln, psr(b7), [("lnv", 0), ("lnv", 1)], bias=eps_t[:, 0:1], scale=1.0 / 128)
                        act(rstd[:], lnv[:], AF.Exp, [("lnv", 0), ("lnv", 1)], ["rstd"], scale=-0.5)
                        stt(OT[:, hb, tsl], ov, sg[:, l:l + 1], rstd[:], ALU.mult, ALU.mult,
                            ["ov", "rstd"], [("OT", hb, qt, 0), ("OT", hb, qt, 1)])
                    return later

                attn_core(l, "b", hb, tslots, [post0, post1])
            w_out_round(l, 1)

        def w_out_round(l, rnd):
            for ocg in range(2):
                slot, W = wtile(4, 512, [("out", l, rnd * 512, ocg * 512, 512, 0)])
                for oc4 in range(4):
                    oc = ocg * 4 + oc4
                    for t4 in range(4):
                        b = (oc4 * 4 + t4) % 3
                        tsl = slice(t4 * 512, (t4 + 1) * 512)
                        for ic in range(4):
                            mm(ps[:, b, :], W[:, ic, oc4 * 128:(oc4 + 1) * 128], OT[:, ic, tsl], ic == 0, ic == 3,
                               [("OT", ic, t4, 0), ("OT", ic, t4, 1), ("w", slot)], psr(b))
                        tt(xT[:, oc, tsl], xT[:, oc, tsl], ps[:, b, :], ALU.add, psr(b) + [("xT", oc, t4)], [("xT", oc, t4)])

        def ffn(l):
            cwb = l * 3 * 44
            for half in range(2):
                jn = 0
                for jg in range(11):
                    nj = 2
                    slot, W = wtile(8, 512, [("up", l, 0, 256 * jg, 256, 0), ("up", l, 0, DFF + 256 * jg, 256, 256)])
                    for jj in range(nj):
                        j = jg * 2 + jj
                        for gv in range(2):
                            b0 = 4 * (jn % 2) + 2 * gv
                            ch = j + NJ * gv
                            wc = 256 * gv + jj * 128
                            for t2 in range(2):
                                t4 = 2 * half + t2
                                for k in range(8):
                                    mm(ps[:, b0 + t2, :], W[:, k, wc:wc + 128], hT[:, k, t4 * 512:(t4 + 1) * 512],
                                       k == 0, k == 7, [("hT", k, t4), ("w", slot)], psr(b0 + t2))
                            pu = ps[:, b0:b0 + 2, :].rearrange("p a t -> p (a t)")
                            a = ct[:, gv, :]
                            w0 = cw[:, cwb + ch:cwb + ch + 1]
                            w1 = cw[:, cwb + 44 + ch:cwb + 44 + ch + 1]
                            w2 = cw[:, cwb + 88 + ch:cwb + 88 + ch + 1]
                            bb = cb[:, l * 44 + ch:l * 44 + ch + 1]
                            pres = psr(b0) + psr(b0 + 1)
                            act(a, pu, AF.Identity, pres, [("ct", gv)], bias=bb, scale=w2)
                            stt(a[:, 1:1024], pu[:, 0:1023], w1, a[:, 1:1024], ALU.mult, ALU.add, pres + [("ct", gv)], [("ct", gv)])
                            stt(a[:, 2:1024], pu[:, 0:1022], w0, a[:, 2:1024], ALU.mult, ALU.add, pres + [("ct", gv)], [("ct", gv)])
                            if half == 0:
                                act(carry[:, ch, :], pu[:, 1022:1024], AF.Copy, pres, [("carry", ch)])
                            else:
                                stt(a[:, 0:2], carry[:, ch, :], w0, a[:, 0:2], ALU.mult, ALU.add,
                                    [("carry", ch), ("ct", gv)], [("ct", gv)])
                                stt(a[:, 0:1], carry[:, ch, 1:2], w1, a[:, 0:1], ALU.mult, ALU.add,
                                    [("carry", ch), ("ct", gv)], [("ct", gv)])
                        act(ct[:, 0, :], ct[:, 0, :], AF.Silu, [("ct", 0)], [("ct", 0)])
                        tt(Gb[:, j, :], ct[:, 0, :], ct[:, 1, :], ALU.mult, [("ct", 0), ("ct", 1)], [("G", j)])
                        jn += 1
                for ocg in range(2):
                    for jg3 in range(3):
                        j0 = 8 * jg3
                        njj = 8 if jg3 < 2 else 6
                        slot, W = wtile(njj, 512, [("down", l, j0 * 128, ocg * 512, 512, 0)])
                        for oc4 in range(4):
                            for t2 in range(2):
                                b = oc4 * 2 + t2
                                for jj in range(njj):
                                    j = j0 + jj
                                    mm(ps[:, b, :], W[:, jj, oc4 * 128:(oc4 + 1) * 128], Gb[:, j, t2 * 512:(t2 + 1) * 512],
                                       j == 0, j == NJ - 1, [("G", j), ("w", slot)], psr(b))
                    for oc4 in range(4):
                        oc = ocg * 4 + oc4
                        for t2 in range(2):
                            b = oc4 * 2 + t2
                            t4 = 2 * half + t2
                            tsl = slice(t4 * 512, (t4 + 1) * 512)
                            tt(xT[:, oc, tsl], xT[:, oc, tsl], ps[:, b, :], ALU.add, psr(b) + [("xT", oc, t4)], [("xT", oc, t4)])

        def body():
            for s in range(nseq):
                load_x(s)
                for l in layers:
                    rmsnorm_to(g1, l * 8, hT, "hT")
                    zero_q_pads()
                    attn_A(l)
                    attn_B(l)
                    rmsnorm_to(g2, l * 8, hT, "hT")
                    ffn(l)
                if final_norm:
                    rmsnorm_to(gf, 0, xT, "xT")
                store_out(s)

        xsem = [S.dma_sem("xs0"), S.dma_sem("xs1")]
        osem = S.dma_sem("osem")
        S.dry = True
        body()
        S.dry = False
        WS.reset()
        TS.reset()
        st_rot[0] = 0
        pt_rot[0] = 0
        prologue()
        S.const_reads = ["g1", "g2", "gf", "cw", "cb", "t5f", "identf", "identb", "onesb", "eps", "carry"] + \
            [("nlam", l) for l in range(DEPTH)] + [("sg", l) for l in range(DEPTH)]
        body()
        S.emit_all(tail_waits=[(osem, S.dma_cum[osem])])
    return nc


def _t5_bucket_np(rel):
    rel = np.asarray(rel, np.int32)
    nb = 16
    ret = np.where(rel > 0, nb, 0)
    n = np.abs(rel)
    max_exact = 8
    is_small = n < max_exact
    nf = np.maximum(n, 1).astype(np.float32)
    large = max_exact + (np.log(nf / np.float32(max_exact)) / np.float32(math.log(128 / max_exact))
                         * np.float32(nb - max_exact)).astype(np.int32)
    large = np.minimum(large, nb - 1)
    return ret + np.where(is_small, n, large)


def _host_prep(inp):
    f32 = np.float32
    k = np.arange(128)[:, None]
    v = np.arange(640)[None, :]
    dist = v - k
    dch = v // 64 - k // 64
    idx = np.clip(dist, -128, 128) + 128
    valid_a = (dch >= 0) & (dch <= 8)
    arb = np.asarray(inp["a_rel_bias"], f32)
    ta = arb[:, :, idx]
    ta = np.where(valid_a[None, None], ta, f32(NEG)).astype(f32)
    bucket = _t5_bucket_np(-dist)
    t5 = np.asarray(inp["t5_bias"], f32)
    tb = np.transpose(t5[bucket], (2, 0, 1))
    valid_b = (k // 64) <= (v // 64)
    tb = np.where(valid_b[None], tb, f32(NEG)).astype(f32)
    t5f = np.ascontiguousarray(np.broadcast_to(t5[15][None, :], (128, 8))).astype(f32)

    def fm(a, nchunk):
        a = np.asarray(a, f32)
        L = a.shape[0]
        return np.ascontiguousarray(a.reshape(L, nchunk, 128).transpose(2, 0, 1).reshape(128, L * nchunk))

    g1 = fm(inp["attn_norm_g"], 8)
    g2 = fm(inp["ffn_norm_g"], 8)
    gf = fm(np.asarray(inp["final_norm_g"], f32)[None], 8)
    cwh = np.asarray(inp["conv_w"], f32).reshape(DEPTH * 3, 44 * 128)
    cw = fm(cwh, 44)
    cb = fm(inp["conv_b"], 44)
    sg = np.ascontiguousarray(np.asarray(inp["subln_g"], f32).T)
    lam = np.concatenate([np.asarray(inp[n], f32).reshape(-1) for n in
                          ("lambda_q1", "lambda_k1", "lambda_q2", "lambda_k2")])
    lam = np.ascontiguousarray(np.broadcast_to(lam[None, :], (128, lam.size))).astype(f32)
    return dict(ta=ta, tb=tb, t5f=t5f, g1=g1, g2=g2, gf=gf, cw=cw, cb=cb, sg=sg, lam=lam)


_NC_CACHE = {}


def kernel(x, attn_norm_g, w_in, a_rel_bias, t5_bias, lambda_q1, lambda_k1, lambda_q2, lambda_k2,
           subln_g, w_out, ffn_norm_g, w_up, conv_w, conv_b, w_down, final_norm_g):
    inp = dict(x=x, attn_norm_g=attn_norm_g, w_in=w_in, a_rel_bias=a_rel_bias, t5_bias=t5_bias,
               lambda_q1=lambda_q1, lambda_k1=lambda_k1, lambda_q2=lambda_q2, lambda_k2=lambda_k2,
               subln_g=subln_g, w_out=w_out, ffn_norm_g=ffn_norm_g, w_up=w_up, conv_w=conv_w,
               conv_b=conv_b, w_down=w_down, final_norm_g=final_norm_g)
    hp = _host_prep(inp)
    shared = dict(
        w_in=np.ascontiguousarray(np.asarray(w_in, np.float32)),
        w_out=np.ascontiguousarray(np.asarray(w_out, np.float32)),
        w_up=np.ascontiguousarray(np.asarray(w_up, np.float32)),
        w_down=np.ascontiguousarray(np.asarray(w_down, np.float32)),
        **hp)
    xs = np.asarray(x, np.float32)
    nc = build_nc()
    in_maps = []
    for c in range(N_CORES):
        m = dict(shared)
        m["x"] = np.ascontiguousarray(xs[c * SEQ_PER_CORE:(c + 1) * SEQ_PER_CORE])
        in_maps.append(m)
    res = run_bass_kernel_spmd(nc, in_maps, core_ids=list(range(N_CORES)))
    out = np.concatenate([np.asarray(r["out"], np.float32) for r in res.results], axis=0)
    return out
```

```python
import math
from contextlib import ExitStack

import numpy as np
import concourse.bass as bass
import concourse.mybir as mybir
from concourse.bass_utils import run_bass_kernel_spmd

F32 = mybir.dt.float32
BF16 = mybir.dt.bfloat16
AF = mybir.ActivationFunctionType
ALU = mybir.AluOpType
AX = mybir.AxisListType

D = 1024
S_LEN = 2048
DEPTH = 4
DFF = 2816
NJ = DFF // 128
NEG = -30000.0
RMS_EPS = 1e-6
N_CORES = 8
SEQ_PER_CORE = 2
NS_W = 3

ENGS = ("pe", "act", "dve", "pool", "sp")


class Op:
    __slots__ = ("eng", "fn", "waits", "signal", "sem", "val", "is_dma", "group")

    def __init__(self, eng, fn, is_dma=False):
        self.eng = eng
        self.fn = fn
        self.waits = []
        self.signal = False
        self.sem = None
        self.val = None
        self.is_dma = is_dma
        self.group = None


class Sched:
    def __init__(self, nc, stack):
        self.nc = nc
        self.stack = stack
        self.ops = {e: [] for e in ENGS}
        self.res = {}
        self.esem = {e: stack.enter_context(nc.semaphore("tl_" + e)) for e in ENGS}
        self.dma_cum = {}
        self.dry = False
        self.const_reads = []
        self.cur_group = {e: None for e in ENGS}
        self.ngroups = 0

    def dma_sem(self, name):
        s = self.stack.enter_context(self.nc.semaphore(name))
        self.dma_cum[s] = 0
        return s

    def add(self, eng, fn, reads=(), writes=(), dma_sem=None):
        if self.dry:
            return None
        op = Op(eng, fn, is_dma=dma_sem is not None)
        deps = {}
        if self.const_reads:
            reads = list(reads) + self.const_reads

        def dep(o, raw):
            if o is None or o is op:
                return
            if not o.is_dma and not op.is_dma and o.eng == eng and eng == "pe":
                return
            deps[id(o)] = o

        res = self.res
        for r in reads:
            st = res.get(r)
            if st is not None:
                dep(st[0], True)
        for w in writes:
            st = res.get(w)
            if st is not None:
                dep(st[0], True)
                for o in st[1].values():
                    dep(o, False)
                for o in st[2]:
                    dep(o, False)
        for r in reads:
            st = res.get(r)
            if st is None:
                st = res[r] = [None, {}, []]
            if op.is_dma:
                st[2].append(op)
            else:
                st[1][eng] = op
        for w in writes:
            res[w] = [op, {}, []]
        for o in deps.values():
            o.signal = True
            op.waits.append(o)
        if dma_sem is not None:
            self.dma_cum[dma_sem] += 16
            op.sem = dma_sem
            op.val = self.dma_cum[dma_sem]
            op.signal = True
        op.group = self.cur_group[eng]
        self.ops[eng].append(op)
        return op

    def begin_group(self, eng):
        self.ngroups += 1
        self.cur_group[eng] = self.ngroups

    def end_group(self, eng):
        self.cur_group[eng] = None

    def finalize(self):
        for e in ENGS:
            n = 0
            for op in self.ops[e]:
                if op.is_dma:
                    continue
                if op.signal:
                    n += 1
                    op.sem = self.esem[e]
                    op.val = n

    def emit(self, e, eng):
        waited = {}
        ops = self.ops[e]
        for idx, op in enumerate(ops):
            need = {}
            members = [op]
            if op.group is not None and (idx == 0 or ops[idx - 1].group != op.group):
                j = idx + 1
                while j < len(ops) and ops[j].group == op.group:
                    members.append(ops[j])
                    j += 1
            for mop in members:
                for o in mop.waits:
                    if o.group is not None and o.group == op.group and o.eng == e:
                        continue
                    k = o.sem
                    if waited.get(k, 0) >= o.val:
                        continue
                    if need.get(k, 0) < o.val:
                        need[k] = o.val
            for k, v in need.items():
                eng.wait_ge(k, v)
                waited[k] = v
            ins = op.fn(eng)
            if op.signal:
                ins.then_inc(op.sem, 16 if op.is_dma else 1)

    def emit_all(self, tail_waits=()):
        nc = self.nc
        self.finalize()
        with nc.Block() as block:
            @block.tensor
            def _(eng):
                self.emit("pe", eng)

            @block.scalar
            def _(eng):
                self.emit("act", eng)

            @block.vector
            def _(eng):
                self.emit("dve", eng)

            @block.gpsimd
            def _(eng):
                self.emit("pool", eng)

            @block.sync
            def _(eng):
                self.emit("sp", eng)
                for (s, v) in tail_waits:
                    eng.wait_ge(s, v)


class Stream:
    def __init__(self, S, name, nslots, issue_fn):
        self.S = S
        self.name = name
        self.n = nslots
        self.issue_fn = issue_fn
        self.descs = []
        self.next_issue = 0
        self.next_get = 0
        self.sems = [S.dma_sem("%s_s%d" % (name, i)) for i in range(nslots)]

    def reset(self):
        self.next_issue = 0
        self.next_get = 0

    def _issue(self, i):
        slot = i % self.n
        for fn in self.issue_fn(slot, self.descs[i]):
            self.S.add("pool", fn, writes=[(self.name, slot)], dma_sem=self.sems[slot])

    def get(self, desc):
        i = self.next_get
        self.next_get += 1
        if self.S.dry:
            self.descs.append(desc)
            return i % self.n
        assert self.descs[i] == desc, (self.name, i, self.descs[i], desc)
        upto = min(len(self.descs), i + self.n - 1)
        while self.next_issue < upto:
            self._issue(self.next_issue)
            self.next_issue += 1
        return i % self.n


def build_nc(layers=(0, 1, 2, 3), final_norm=True, nseq=SEQ_PER_CORE):
    nc = bass.Bass("TRN2", target_bir_lowering=False)
    dt = lambda name, shape, kind="ExternalInput": nc.dram_tensor(name, list(shape), F32, kind=kind).ap()
    x_d = dt("x", [nseq, S_LEN, D])
    w_in_d = dt("w_in", [DEPTH, D, 3072])
    w_out_d = dt("w_out", [DEPTH, D, D])
    w_up_d = dt("w_up", [DEPTH, D, 2 * DFF])
    w_down_d = dt("w_down", [DEPTH, DFF, D])
    ta_d = dt("ta", [DEPTH, 8, 128, 640])
    tb_d = dt("tb", [8, 128, 640])
    g1_d = dt("g1", [128, DEPTH * 8])
    g2_d = dt("g2", [128, DEPTH * 8])
    gf_d = dt("gf", [128, 8])
    cw_d = dt("cw", [128, DEPTH * 3 * 44])
    cb_d = dt("cb", [128, DEPTH * 44])
    sg_d = dt("sg", [128, DEPTH])
    lam_d = dt("lam", [128, 4 * DEPTH * 64])
    t5f_d = dt("t5f", [128, 8])
    out_d = dt("out", [nseq, S_LEN, D], kind="ExternalOutput")

    with ExitStack() as st:
        S = Sched(nc, st)
        sb = lambda n, s, d: st.enter_context(nc.sbuf_tensor(n, list(s), d))
        xT = sb("xT", [128, 8, S_LEN], F32)
        hT = sb("hT", [128, 8, S_LEN], BF16)
        U = sb("U", [128, 22528], BF16)
        ct = sb("ct", [128, 2, 1024], F32)
        PT = sb("PT", [128, 8, 512], BF16)
        TAB = sb("TAB", [128, 4, 640], BF16)
        wr = sb("wr", [128, NS_W, 4096], BF16)
        sq = sb("sq", [128, 2, 512], BF16)
        lnv = sb("lnv", [128, 512], F32)
        rstd = sb("rstd", [128, 512], F32)
        scr = sb("scr", [128, 5, 512], F32)
        r0 = scr[:, 0, :]
        r1 = scr[:, 1, :]
        av = scr[:, 2, :]
        bv = scr[:, 3, :]
        ov = scr[:, 4, :]
        scrf = scr[:, :, :].rearrange("p a t -> p (a t)")
        lamr = scrf[:, 0:4 * DEPTH * 64]
        lamp = scrf[:, 4 * DEPTH * 64:6 * DEPTH * 64]
        carry = sb("carry", [128, 44, 2], F32)
        identb = sb("identb", [128, 128], BF16)
        identf = sb("identf", [128, 128], F32)
        onesb = sb("onesb", [128, 128], BF16)
        eps_t = sb("eps_t", [128, 1], F32)
        g1 = sb("g1s", [128, DEPTH * 8], F32)
        g2 = sb("g2s", [128, DEPTH * 8], F32)
        gf = sb("gfs", [128, 8], F32)
        cw = sb("cws", [128, DEPTH * 3 * 44], F32)
        cb = sb("cbs", [128, DEPTH * 44], F32)
        sg = sb("sgs", [128, DEPTH], F32)
        lams = sb("lams", [128, 2 * DEPTH], F32)
        nlam = sb("nlam", [128, DEPTH], F32)
        t5f = sb("t5fs", [128, 8], F32)
        ps = st.enter_context(nc.psum_tensor("ps", [128, 8, 512], F32))

        OT = U[:, 0:8192].rearrange("p (c t) -> p c t", c=4)
        Vb = U[:, 8192:16384].rearrange("p (i n) -> p i n", i=16)
        QT0 = U[:, 16384:18432]
        KT = U[:, 18432:20480]
        QT1 = U[:, 20480:22528]
        QTS = (QT0, QT1)
        Gb = U[:, 0:22528].rearrange("p (j t) -> p j t", j=NJ)

        def psr(b):
            return [("ps", b, 0), ("ps", b, 1)]

        def w_issue(slot, desc):
            fns = []
            kc, ntot, parts = desc[0], desc[1], desc[2]
            dst = wr[:, slot, 0:kc * ntot].rearrange("p (c n) -> p c n", c=kc)
            for (which, l, r0_, c0, n, off) in parts:
                src_t = {"in": w_in_d, "out": w_out_d, "up": w_up_d, "down": w_down_d}[which]
                src = src_t[l, r0_:r0_ + kc * 128, c0:c0 + n].rearrange("(c p) n -> p c n", p=128)
                fns.append((lambda d_, s_: (lambda e: e.dma_start(out=d_, in_=s_)))(dst[:, :, off:off + n], src))
            return fns

        def tab_issue(slot, desc):
            kind, l, h = desc
            src = ta_d[l, h] if kind == "a" else tb_d[h]
            return [(lambda d_, s_: (lambda e: e.dma_start(out=d_, in_=s_)))(TAB[:, slot, :], src)]

        WS = Stream(S, "w", NS_W, w_issue)
        TS = Stream(S, "tab", 4, tab_issue)

        def wtile(kc, ntot, parts):
            slot = WS.get((kc, ntot, tuple(parts)))
            view = wr[:, slot, 0:kc * ntot].rearrange("p (c n) -> p c n", c=kc)
            return slot, view

        def mm(out, lhsT, rhs, start, stop, reads, writes):
            S.add("pe", lambda e: e.matmul(out, lhsT, rhs, start=start, stop=stop), reads=reads, writes=writes)

        def tr(out, in_, reads, writes):
            S.add("pe", lambda e: e.transpose(out, in_, identf[:]), reads=reads, writes=writes)

        def act(out, in_, func, reads, writes, **kw):
            S.add("act", lambda e: e.activation(out, in_, func, **kw), reads=reads, writes=writes)

        def tt(out, a, b, op, reads, writes, eng="dve"):
            S.add(eng, lambda e: e.tensor_tensor(out, a, b, op), reads=reads, writes=writes)

        def stt(out, in0, scalar, in1, op0, op1, reads, writes, eng="dve"):
            S.add(eng, lambda e: e.scalar_tensor_tensor(out, in0, scalar, in1, op0, op1), reads=reads, writes=writes)

        def ts1(out, in0, scalar, op, reads, writes):
            S.add("dve", lambda e: e.tensor_scalar(out, in0, scalar, None, op), reads=reads, writes=writes)

        def cp(out, in_, reads, writes, eng="dve"):
            S.add(eng, lambda e: e.tensor_copy(out, in_), reads=reads, writes=writes)

        def recip(out, in_, reads, writes):
            S.add("dve", lambda e: e.reciprocal(out, in_), reads=reads, writes=writes)

        def dma(eng, out, in_, reads, writes, sem):
            return S.add(eng, lambda e: e.dma_start(out=out, in_=in_), reads=reads, writes=writes, dma_sem=sem)

        def prologue():
            dc = S.dma_sem("dconst")
            cops = []
            for (dst, src, nm) in [(g1[:], g1_d, "g1"), (g2[:], g2_d, "g2"), (gf[:], gf_d, "gf"), (cw[:], cw_d, "cw"),
                                   (cb[:], cb_d, "cb"), (sg[:], sg_d, "sg"), (lamr, lam_d, "lamr"), (t5f[:], t5f_d, "t5f")]:
                cops.append(dma("sp", dst, src, [], [nm], dc))
            for o in cops:
                o.val = S.dma_cum[dc]
            S.add("pool", lambda e: e.memset(identf[:], 0.0), writes=["identf"])
            S.add("pool", lambda e: e.affine_select(out=identf[:], in_=identf[:], pattern=[[-1, 128]],
                                                    compare_op=ALU.not_equal, fill=1.0, base=0,
                                                    channel_multiplier=1),
                  reads=["identf"], writes=["identf"])
            S.add("pool", lambda e: e.memset(onesb[:], 1.0), writes=["onesb"])
            S.add("pool", lambda e: e.memset(eps_t[:], RMS_EPS), writes=["eps"])
            S.add("pool", lambda e: e.memset(carry[:], 0.0), writes=["carry"])
            cp(identb[:], identf[:], ["identf"], ["identb"])
            n64 = DEPTH * 64
            tt(lamp[:, 0:n64], lamr[:, 0:n64], lamr[:, n64:2 * n64], ALU.mult, ["lamr"], ["lamp0"])
            tt(lamp[:, n64:2 * n64], lamr[:, 2 * n64:3 * n64], lamr[:, 3 * n64:4 * n64], ALU.mult, ["lamr"], ["lamp1"])
            for i in range(2 * DEPTH):
                S.add("dve", (lambda o_, i_: (lambda e: e.tensor_reduce(o_, i_, AX.X, ALU.add)))(
                    lams[:, i:i + 1], lamp[:, i * 64:(i + 1) * 64]),
                      reads=["lamp0", "lamp1"], writes=[("lams", i)])
            act(lams[:], lams[:], AF.Exp, [("lams", i) for i in range(2 * DEPTH)], ["lamse"])
            tt(nlam[:], lams[:, DEPTH:2 * DEPTH], lams[:, 0:DEPTH], ALU.subtract, ["lamse"], ["nlam0"])
            for l in range(DEPTH):
                li = 0.8 - 0.6 * math.exp(-0.3 * l)
                ts1(nlam[:, l:l + 1], nlam[:, l:l + 1], -li, ALU.add, ["nlam0"], [("nlam", l)])
                ts1(sg[:, l:l + 1], sg[:, l:l + 1], 1.0 - li, ALU.mult, ["sg"], [("sg", l)])

        def rmsnorm_to(gain_tile, gcol0, dst_t, dst_name, after_tile=None):
            for t4 in range(4):
                if after_tile is not None and t4 >= 2:
                    after_tile(t4 - 2)
                tsl = slice(t4 * 512, (t4 + 1) * 512)
                for c in range(8):
                    sl = c % 2
                    act(sq[:, sl, :], xT[:, c, tsl], AF.Square, [("xT", c, t4)], [("sq", sl)])
                    mm(ps[:, 7, :], onesb[:], sq[:, sl, :], c == 0, c == 7, [("sq", sl)], psr(7))
                act(lnv[:], ps[:, 7, :], AF.Ln, psr(7), [("lnv", 0), ("lnv", 1)], bias=eps_t[:, 0:1], scale=1.0 / D)
                act(rstd[:], lnv[:], AF.Exp, [("lnv", 0), ("lnv", 1)], ["rstd"], scale=-0.5)
                for c in range(8):
                    stt(dst_t[:, c, tsl], xT[:, c, tsl], gain_tile[:, gcol0 + c:gcol0 + c + 1], rstd[:],
                        ALU.mult, ALU.mult, [("xT", c, t4), "rstd"], [(dst_name, c, t4)])
            if after_tile is not None:
                after_tile(2)
                after_tile(3)

        def load_x(s):
            for i in range(16):
                sl = i % 2
                dma("sp", ct[:, sl, :], x_d[s, i * 128:(i + 1) * 128, :], [], [("ct", sl)], xsem[sl])
                for half in range(2):
                    b = 2 * (i % 2) + half
                    for c4 in range(4):
                        c = half * 4 + c4
                        tr(ps[:, b, c4 * 128:(c4 + 1) * 128], ct[:, sl, c * 128:(c + 1) * 128], [("ct", sl)], psr(b))
                    dst = xT[:, half * 4:half * 4 + 4, i * 128:(i + 1) * 128]
                    src = ps[:, b, :].rearrange("p (c t) -> p c t", c=4)
                    wr_ = [("xT", half * 4 + c4, i // 4) for c4 in range(4)]
                    if half == 0:
                        act(dst, src, AF.Copy, psr(b), wr_)
                    else:
                        cp(dst, src, psr(b), wr_)

        def store_out(s):
            for i in range(16):
                sl = i % 2
                for half in range(2):
                    b = 2 * (i % 2) + half
                    for c4 in range(4):
                        c = half * 4 + c4
                        tr(ps[:, b, c4 * 128:(c4 + 1) * 128], xT[:, c, i * 128:(i + 1) * 128], [("xT", c, i // 4)], psr(b))
                    dst = ct[:, sl, half * 512:(half + 1) * 512]
                    if half == 0:
                        act(dst, ps[:, b, :], AF.Copy, psr(b) + [("ct", sl)], [("ct", sl, half)])
                    else:
                        cp(dst, ps[:, b, :], psr(b) + [("ct", sl)], [("ct", sl, half)])
                dma("sp", out_d[s, i * 128:(i + 1) * 128, :], ct[:, sl, :],
                    [("ct", sl, 0), ("ct", sl, 1)], [("ct", sl)], osem)

        def proj_v(l, c0, chunks=None):
            slot, W = wtile(8, 512, [("in", l, 0, c0, 512, 0)])
            if chunks is not None:
                def chunk(t4):
                    for i in range(4 * t4, 4 * t4 + 4):
                        b = i % 3
                        for k in range(8):
                            mm(ps[:, b, :], hT[:, k, i * 128:(i + 1) * 128], W[:, k, :], k == 0, k == 7,
                               [("hT", k, i // 4), ("w", slot)], psr(b))
                        cp(Vb[:, i, :], ps[:, b, :], psr(b), [("V", i)])
                return chunk
            for i in range(16):
                b = i % 3
                for k in range(8):
                    mm(ps[:, b, :], hT[:, k, i * 128:(i + 1) * 128], W[:, k, :], k == 0, k == 7,
                       [("hT", k, i // 4), ("w", slot)], psr(b))
                cp(Vb[:, i, :], ps[:, b, :], psr(b), [("V", i)])

        def proj_qk(l, cq, ck):
            slot, W = wtile(8, 256, [("in", l, 0, cq, 128, 0), ("in", l, 0, ck, 128, 128)])
            for which in range(2):
                for t4 in range(4):
                    b = (which * 4 + t4) % 3
                    for k in range(8):
                        mm(ps[:, b, :], W[:, k, which * 128:(which + 1) * 128], hT[:, k, t4 * 512:(t4 + 1) * 512],
                           k == 0, k == 7, [("hT", k, t4), ("w", slot)], psr(b))
                    tsl = slice(t4 * 512, (t4 + 1) * 512)
                    if which == 0:
                        ts1(QT0[0:64, tsl], ps[0:64, b, :], 0.125, ALU.mult, psr(b), [("QT", 0, t4)])
                        ts1(QT1[64:128, tsl], ps[64:128, b, :], 0.125, ALU.mult, psr(b), [("QT", 1, t4)])
                    else:
                        cp(KT[:, tsl], ps[:, b, :], psr(b), [("KT", t4)])

        st_rot = [0]
        pt_rot = [0]
        ST_BANKS = (0, 1, 2, 7)
        PIPE_DEPTH = 3

        def next_st():
            b = ST_BANKS[st_rot[0] % 4]
            st_rot[0] += 1
            return b

        def attn_steps_run(steps):
            n = len(steps)
            deferred = []
            for i in range(n + PIPE_DEPTH):
                S.begin_group("pe")
                if i < n:
                    steps[i][0]()
                if i >= PIPE_DEPTH:
                    steps[i - PIPE_DEPTH][1]()
                S.end_group("pe")
                for dq in deferred:
                    dq[0] -= 1
                while deferred and deferred[0][0] <= 0:
                    deferred.pop(0)[1]()
                if i >= PIPE_DEPTH and steps[i - PIPE_DEPTH][2] is not None:
                    later = steps[i - PIPE_DEPTH][2]()
                    if later is not None:
                        deferred.append([8, later])
            while deferred:
                deferred.pop(0)[1]()

        def score_step(kt_ap, qt_ap, c0, c1, tab_ap, tab_res, exp_bias, qk_reads, pv_list, post=None):
            def qk():
                b = next_st()
                pslot = pt_rot[0] % 8
                pt_rot[0] += 1
                st_[0] = (b, pslot)
                mm(ps[:, b, c0:c1], kt_ap, qt_ap, True, tab_ap is None, qk_reads, psr(b))
                if tab_ap is not None:
                    mm(ps[:, b, c0:c1], identb[:], tab_ap, False, True, [tab_res], psr(b))
                if exp_bias is None:
                    act(PT[:, pslot, c0:c1], ps[:, b, c0:c1], AF.Exp, psr(b), [("PT", pslot)])
                else:
                    act(PT[:, pslot, c0:c1], ps[:, b, c0:c1], AF.Exp, psr(b), [("PT", pslot)], bias=exp_bias)

            st_ = [None]

            def pv():
                b, pslot = st_[0]
                for (out_ap, lhsT_ap, start, stop, reads, writes) in pv_list:
                    mm(out_ap, lhsT_ap, PT[:, pslot, c0:c1], start, stop, [("PT", pslot)] + reads, writes)

            return (qk, pv, post)

        def recip_act(out, in_ps, in_res, out_res):
            act(lnv[:], in_ps, AF.Ln, in_res, ["lnv"])
            act(out, lnv[:], AF.Exp, ["lnv"], [out_res], scale=-1.0)

        def zero_q_pads():
            S.add("dve", lambda e: e.memset(QT0[64:128, :], 0.0), writes=[("G", 16), ("G", 17), ("QTz", 0)])
            S.add("dve", lambda e: e.memset(QT1[0:64, :], 0.0), writes=[("G", 20), ("G", 21), ("QTz", 1)])

        def attn_core(l, kind, unit, tslots, post_fns):
            steps = []
            for qt in range(4):
                for hh in range(2):
                    bO, bS = 3 + 2 * hh, 4 + 2 * hh
                    if kind == "a":
                        rs = [r for r in (4, 3, 5, 2, 6, 1, 7, 0) if 4 * qt - 4 + r >= 0]
                        blocks = []
                        for r in rs:
                            kb = 4 * qt - 4 + r
                            lo, hi = max(0, 2 * r - 8), min(7, 2 * r + 1)
                            c0, c1 = 64 * lo, 64 * (hi + 1)
                            v0 = 512 - 128 * r
                            blocks.append((kb, c0, c1, TAB[:, tslots[hh], v0 + c0:v0 + c1], ("tab", tslots[hh]), None))
                        vcol = slice(unit * 128, (unit + 1) * 128)
                    else:
                        m = 2 * unit + hh
                        blocks = []
                        kbs = list(range(4 * qt, 4 * qt + 4)) + ([4 * qt - 1] if qt > 0 else []) + list(range(0, max(0, 4 * qt - 1)))
                        for kb in kbs:
                            j = 4 * qt - kb
                            if j >= 2:
                                blocks.append((kb, 0, 512, None, None, t5f[:, m:m + 1]))
                            elif j == 1:
                                blocks.append((kb, 0, 512, TAB[:, tslots[hh], 128:640], ("tab", tslots[hh]), None))
                            else:
                                c0 = 128 * (-j)
                                blocks.append((kb, c0, 512, TAB[:, tslots[hh], 0:512 - c0], ("tab", tslots[hh]), None))
                        vcol = slice(unit * 128, (unit + 1) * 128)
                    nb = len(blocks)
                    for idx, (kb, c0, c1, tab_ap, tab_res, ebias) in enumerate(blocks):
                        first, last = idx == 0, idx == nb - 1
                        pv_list = [
                            (ps[:, bO, c0:c1], Vb[:, kb, vcol], first, last, [("V", kb)], psr(bO)),
                            (ps[:, bS, c0:c1], onesb[:], first, last, [], psr(bS)),
                        ]
                        post = None
                        if last:
                            post = (lambda f=post_fns[hh], qt_=qt: f(qt_))
                        steps.append(score_step(
                            KT[:, kb * 128:(kb + 1) * 128],
                            QTS[hh][:, qt * 512 + c0:qt * 512 + c1],
                            c0, c1, tab_ap, tab_res, ebias, [("KT", kb // 4), ("QT", hh, qt), ("QTz", hh)], pv_list,
                            post=post))
            attn_steps_run(steps)

        def attn_A(l, v_done=False):
            if not v_done:
                proj_v(l, 1024)
            for p in range(4):
                proj_qk(l, 128 * p, 512 + 128 * p)
                tslots = [TS.get(("a", l, 2 * p + hh)) for hh in range(2)]

                def mk_post(hh, p=p):
                    lo_, hi_ = 64 * hh, 64 * hh + 64
                    bO, bS = 3 + 2 * hh, 4 + 2 * hh
                    rr = r0 if hh == 0 else r1

                    def post(qt):
                        tsl = slice(qt * 512, (qt + 1) * 512)
                        act(lnv[lo_:hi_, :], ps[lo_:hi_, bS, :], AF.Ln, psr(bS), [("lnv", hh)])
                        act(rr[lo_:hi_, :], lnv[lo_:hi_, :], AF.Exp, [("lnv", hh)], [("rr", hh)], scale=-1.0)
                        tt(OT[lo_:hi_, p, tsl], ps[lo_:hi_, bO, :], rr[lo_:hi_, :], ALU.mult,
                           psr(bO) + [("rr", hh)], [("OT", p, qt, hh)])
                    return post

                attn_core(l, "a", p, tslots, [mk_post(0), mk_post(1)])
            w_out_round(l, 0)

        def attn_B(l):
            proj_v(l, 2560)
            for hb in range(4):
                proj_qk(l, 1536 + 128 * hb, 2048 + 128 * hb)
                tslots = [TS.get(("b", 0, 2 * hb + mmi)) for mmi in range(2)]

                def post0(qt):
                    act(lnv[:], ps[:, 4, :], AF.Ln, psr(4), [("lnv", 0), ("lnv", 1)])
                    act(r0, lnv[:], AF.Exp, [("lnv", 0), ("lnv", 1)], [("rr", 0)], scale=-1.0)
                    tt(av, ps[:, 3, :], r0, ALU.mult, psr(3) + [("rr", 0)], ["av"])

                def post1(qt, hb=hb):
                    tsl = slice(qt * 512, (qt + 1) * 512)
                    act(lnv[:], ps[:, 6, :], AF.Ln, psr(6), [("lnv", 0), ("lnv", 1)])
                    act(r1, lnv[:], AF.Exp, [("lnv", 0), ("lnv", 1)], [("rr", 1)], scale=-1.0)
                    stt(bv, ps[:, 5, :], nlam[:, l:l + 1], r1, ALU.mult, ALU.mult, psr(5) + [("rr", 1)], ["bv"])
                    tt(ov, av, bv, ALU.add, ["av", "bv"], ["ov"])
                    tt(sq[:, 0, :], ov, ov, ALU.mult, ["ov"], [("sq", 0)])

                    def later():
                        b7 = next_st()
                        mm(ps[:, b7, :], onesb[:], sq[:, 0, :], True, True, [("sq", 0)], psr(b7))
                        act(lnv[:], ps[:, b7, :], AF.Ln, psr(b7), [("lnv", 0), ("lnv", 1)], bias=eps_t[:, 0:1], scale=1.0 / 128)
                        act(rstd[:], lnv[:], AF.Exp, [("lnv", 0), ("lnv", 1)], ["rstd"], scale=-0.5)
                        stt(OT[:, hb, tsl], ov, sg[:, l:l + 1], rstd[:], ALU.mult, ALU.mult,
                            ["ov", "rstd"], [("OT", hb, qt, 0), ("OT", hb, qt, 1)])
                    return later

                attn_core(l, "b", hb, tslots, [post0, post1])
            w_out_round(l, 1)

        def w_out_round(l, rnd):
            for ocg in range(2):
                slot, W = wtile(4, 512, [("out", l, rnd * 512, ocg * 512, 512, 0)])
                for oc4 in range(4):
                    oc = ocg * 4 + oc4
                    for t4 in range(4):
                        b = (oc4 * 4 + t4) % 3
                        tsl = slice(t4 * 512, (t4 + 1) * 512)
                        for ic in range(4):
                            mm(ps[:, b, :], W[:, ic, oc4 * 128:(oc4 + 1) * 128], OT[:, ic, tsl], ic == 0, ic == 3,
                               [("OT", ic, t4, 0), ("OT", ic, t4, 1), ("w", slot)], psr(b))
                        tt(xT[:, oc, tsl], xT[:, oc, tsl], ps[:, b, :], ALU.add, psr(b) + [("xT", oc, t4)], [("xT", oc, t4)])

        def ffn(l):
            cwb = l * 3 * 44
            for half in range(2):
                jn = 0
                for jg in range(11):
                    nj = 2
                    slot, W = wtile(8, 512, [("up", l, 0, 256 * jg, 256, 0), ("up", l, 0, DFF + 256 * jg, 256, 256)])
                    for jj in range(nj):
                        j = jg * 2 + jj
                        for gv in range(2):
                            b0 = 4 * (jn % 2) + 2 * gv
                            ch = j + NJ * gv
                            wc = 256 * gv + jj * 128
                            for t2 in range(2):
                                t4 = 2 * half + t2
                                for k in range(8):
                                    mm(ps[:, b0 + t2, :], W[:, k, wc:wc + 128], hT[:, k, t4 * 512:(t4 + 1) * 512],
                                       k == 0, k == 7, [("hT", k, t4), ("w", slot)], psr(b0 + t2))
                            pu = ps[:, b0:b0 + 2, :].rearrange("p a t -> p (a t)")
                            a = ct[:, gv, :]
                            w0 = cw[:, cwb + ch:cwb + ch + 1]
                            w1 = cw[:, cwb + 44 + ch:cwb + 44 + ch + 1]
                            w2 = cw[:, cwb + 88 + ch:cwb + 88 + ch + 1]
                            bb = cb[:, l * 44 + ch:l * 44 + ch + 1]
                            pres = psr(b0) + psr(b0 + 1)
                            act(a, pu, AF.Identity, pres, [("ct", gv)], bias=bb, scale=w2)
                            stt(a[:, 1:1024], pu[:, 0:1023], w1, a[:, 1:1024], ALU.mult, ALU.add, pres + [("ct", gv)], [("ct", gv)])
                            stt(a[:, 2:1024], pu[:, 0:1022], w0, a[:, 2:1024], ALU.mult, ALU.add, pres + [("ct", gv)], [("ct", gv)])
                            if half == 0:
                                act(carry[:, ch, :], pu[:, 1022:1024], AF.Copy, pres, [("carry", ch)])
                            else:
                                stt(a[:, 0:2], carry[:, ch, :], w0, a[:, 0:2], ALU.mult, ALU.add,
                                    [("carry", ch), ("ct", gv)], [("ct", gv)])
                                stt(a[:, 0:1], carry[:, ch, 1:2], w1, a[:, 0:1], ALU.mult, ALU.add,
                                    [("carry", ch), ("ct", gv)], [("ct", gv)])
                        act(ct[:, 0, :], ct[:, 0, :], AF.Silu, [("ct", 0)], [("ct", 0)])
                        tt(Gb[:, j, :], ct[:, 0, :], ct[:, 1, :], ALU.mult, [("ct", 0), ("ct", 1)], [("G", j)])
                        jn += 1
                for ocg in range(2):
                    for jg3 in range(3):
                        j0 = 8 * jg3
                        njj = 8 if jg3 < 2 else 6
                        slot, W = wtile(njj, 512, [("down", l, j0 * 128, ocg * 512, 512, 0)])
                        for oc4 in range(4):
                            for t2 in range(2):
                                b = oc4 * 2 + t2
                                for jj in range(njj):
                                    j = j0 + jj
                                    mm(ps[:, b, :], W[:, jj, oc4 * 128:(oc4 + 1) * 128], Gb[:, j, t2 * 512:(t2 + 1) * 512],
                                       j == 0, j == NJ - 1, [("G", j), ("w", slot)], psr(b))
                    for oc4 in range(4):
                        oc = ocg * 4 + oc4
                        for t2 in range(2):
                            b = oc4 * 2 + t2
                            t4 = 2 * half + t2
                            tsl = slice(t4 * 512, (t4 + 1) * 512)
                            tt(xT[:, oc, tsl], xT[:, oc, tsl], ps[:, b, :], ALU.add, psr(b) + [("xT", oc, t4)], [("xT", oc, t4)])

        def body():
            for s in range(nseq):
                load_x(s)
                for l in layers:
                    vchunk = proj_v(l, 1024, chunks=True)
                    rmsnorm_to(g1, l * 8, hT, "hT", after_tile=vchunk)
                    zero_q_pads()
                    attn_A(l, v_done=True)
                    attn_B(l)
                    rmsnorm_to(g2, l * 8, hT, "hT")
                    ffn(l)
                if final_norm:
                    rmsnorm_to(gf, 0, xT, "xT")
                store_out(s)

        xsem = [S.dma_sem("xs0"), S.dma_sem("xs1")]
        osem = S.dma_sem("osem")
        S.dry = True
        body()
        S.dry = False
        WS.reset()
        TS.reset()
        st_rot[0] = 0
        pt_rot[0] = 0
        prologue()
        S.const_reads = ["g1", "g2", "gf", "cw", "cb", "t5f", "identf", "identb", "onesb", "eps", "carry"] + \
            [("nlam", l) for l in range(DEPTH)] + [("sg", l) for l in range(DEPTH)]
        body()
        S.emit_all(tail_waits=[(osem, S.dma_cum[osem])])
    return nc


def _t5_bucket_np(rel):
    rel = np.asarray(rel, np.int32)
    nb = 16
    ret = np.where(rel > 0, nb, 0)
    n = np.abs(rel)
    max_exact = 8
    is_small = n < max_exact
    nf = np.maximum(n, 1).astype(np.float32)
    large = max_exact + (np.log(nf / np.float32(max_exact)) / np.float32(math.log(128 / max_exact))
                         * np.float32(nb - max_exact)).astype(np.int32)
    large = np.minimum(large, nb - 1)
    return ret + np.where(is_small, n, large)


def _host_prep(inp):
    f32 = np.float32
    k = np.arange(128)[:, None]
    v = np.arange(640)[None, :]
    dist = v - k
    dch = v // 64 - k // 64
    idx = np.clip(dist, -128, 128) + 128
    valid_a = (dch >= 0) & (dch <= 8)
    arb = np.asarray(inp["a_rel_bias"], f32)
    ta = arb[:, :, idx]
    ta = np.where(valid_a[None, None], ta, f32(NEG)).astype(f32)
    bucket = _t5_bucket_np(-dist)
    t5 = np.asarray(inp["t5_bias"], f32)
    tb = np.transpose(t5[bucket], (2, 0, 1))
    valid_b = (k // 64) <= (v // 64)
    tb = np.where(valid_b[None], tb, f32(NEG)).astype(f32)
    t5f = np.ascontiguousarray(np.broadcast_to(t5[15][None, :], (128, 8))).astype(f32)

    def fm(a, nchunk):
        a = np.asarray(a, f32)
        L = a.shape[0]
        return np.ascontiguousarray(a.reshape(L, nchunk, 128).transpose(2, 0, 1).reshape(128, L * nchunk))

    g1 = fm(inp["attn_norm_g"], 8)
    g2 = fm(inp["ffn_norm_g"], 8)
    gf = fm(np.asarray(inp["final_norm_g"], f32)[None], 8)
    cwh = np.asarray(inp["conv_w"], f32).reshape(DEPTH * 3, 44 * 128)
    cw = fm(cwh, 44)
    cb = fm(inp["conv_b"], 44)
    sg = np.ascontiguousarray(np.asarray(inp["subln_g"], f32).T)
    lam = np.concatenate([np.asarray(inp[n], f32).reshape(-1) for n in
                          ("lambda_q1", "lambda_k1", "lambda_q2", "lambda_k2")])
    lam = np.ascontiguousarray(np.broadcast_to(lam[None, :], (128, lam.size))).astype(f32)
    return dict(ta=ta, tb=tb, t5f=t5f, g1=g1, g2=g2, gf=gf, cw=cw, cb=cb, sg=sg, lam=lam)


_NC_CACHE = {}


def kernel(x, attn_norm_g, w_in, a_rel_bias, t5_bias, lambda_q1, lambda_k1, lambda_q2, lambda_k2,
           subln_g, w_out, ffn_norm_g, w_up, conv_w, conv_b, w_down, final_norm_g):
    inp = dict(x=x, attn_norm_g=attn_norm_g, w_in=w_in, a_rel_bias=a_rel_bias, t5_bias=t5_bias,
               lambda_q1=lambda_q1, lambda_k1=lambda_k1, lambda_q2=lambda_q2, lambda_k2=lambda_k2,
               subln_g=subln_g, w_out=w_out, ffn_norm_g=ffn_norm_g, w_up=w_up, conv_w=conv_w,
               conv_b=conv_b, w_down=w_down, final_norm_g=final_norm_g)
    hp = _host_prep(inp)
    shared = dict(
        w_in=np.ascontiguousarray(np.asarray(w_in, np.float32)),
        w_out=np.ascontiguousarray(np.asarray(w_out, np.float32)),
        w_up=np.ascontiguousarray(np.asarray(w_up, np.float32)),
        w_down=np.ascontiguousarray(np.asarray(w_down, np.float32)),
        **hp)
    xs = np.asarray(x, np.float32)
    nc = build_nc()
    in_maps = []
    for c in range(N_CORES):
        m = dict(shared)
        m["x"] = np.ascontiguousarray(xs[c * SEQ_PER_CORE:(c + 1) * SEQ_PER_CORE])
        in_maps.append(m)
    res = run_bass_kernel_spmd(nc, in_maps, core_ids=list(range(N_CORES)))
    out = np.concatenate([np.asarray(r["out"], np.float32) for r in res.results], axis=0)
    return out
```

```python
import math
from contextlib import ExitStack

import numpy as np
import concourse.bass as bass
import concourse.mybir as mybir
from concourse.bass_utils import run_bass_kernel_spmd

F32 = mybir.dt.float32
BF16 = mybir.dt.bfloat16
AF = mybir.ActivationFunctionType
ALU = mybir.AluOpType
AX = mybir.AxisListType

D = 1024
S_LEN = 2048
DEPTH = 4
DFF = 2816
NJ = DFF // 128
NEG = -30000.0
RMS_EPS = 1e-6
N_CORES = 8
SEQ_PER_CORE = 2
NS_W = 3

ENGS = ("pe", "act", "dve", "pool", "sp")


class Op:
    __slots__ = ("eng", "fn", "waits", "signal", "sem", "val", "is_dma", "group")

    def __init__(self, eng, fn, is_dma=False):
        self.eng = eng
        self.fn = fn
        self.waits = []
        self.signal = False
        self.sem = None
        self.val = None
        self.is_dma = is_dma
        self.group = None


class Sched:
    def __init__(self, nc, stack):
        self.nc = nc
        self.stack = stack
        self.ops = {e: [] for e in ENGS}
        self.res = {}
        self.esem = {e: stack.enter_context(nc.semaphore("tl_" + e)) for e in ENGS}
        self.dma_cum = {}
        self.dry = False
        self.const_reads = []
        self.cur_group = {e: None for e in ENGS}
        self.ngroups = 0

    def dma_sem(self, name):
        s = self.stack.enter_context(self.nc.semaphore(name))
        self.dma_cum[s] = 0
        return s

    def add(self, eng, fn, reads=(), writes=(), dma_sem=None):
        if self.dry:
            return None
        op = Op(eng, fn, is_dma=dma_sem is not None)
        deps = {}
        if self.const_reads:
            reads = list(reads) + self.const_reads

        def dep(o, raw):
            if o is None or o is op:
                return
            if not o.is_dma and not op.is_dma and o.eng == eng and eng == "pe":
                return
            deps[id(o)] = o

        res = self.res
        for r in reads:
            st = res.get(r)
            if st is not None:
                dep(st[0], True)
        for w in writes:
            st = res.get(w)
            if st is not None:
                dep(st[0], True)
                for o in st[1].values():
                    dep(o, False)
                for o in st[2]:
                    dep(o, False)
        for r in reads:
            st = res.get(r)
            if st is None:
                st = res[r] = [None, {}, []]
            if op.is_dma:
                st[2].append(op)
            else:
                st[1][eng] = op
        for w in writes:
            res[w] = [op, {}, []]
        for o in deps.values():
            o.signal = True
            op.waits.append(o)
        if dma_sem is not None:
            self.dma_cum[dma_sem] += 16
            op.sem = dma_sem
            op.val = self.dma_cum[dma_sem]
            op.signal = True
        op.group = self.cur_group[eng]
        self.ops[eng].append(op)
        return op

    def begin_group(self, eng):
        self.ngroups += 1
        self.cur_group[eng] = self.ngroups

    def end_group(self, eng):
        self.cur_group[eng] = None

    def finalize(self):
        for e in ENGS:
            n = 0
            for op in self.ops[e]:
                if op.is_dma:
                    continue
                if op.signal:
                    n += 1
                    op.sem = self.esem[e]
                    op.val = n

    def emit(self, e, eng):
        waited = {}
        ops = self.ops[e]
        for idx, op in enumerate(ops):
            need = {}
            members = [op]
            if op.group is not None and (idx == 0 or ops[idx - 1].group != op.group):
                j = idx + 1
                while j < len(ops) and ops[j].group == op.group:
                    members.append(ops[j])
                    j += 1
            for mop in members:
                for o in mop.waits:
                    if o.group is not None and o.group == op.group and o.eng == e:
                        continue
                    k = o.sem
                    if waited.get(k, 0) >= o.val:
                        continue
                    if need.get(k, 0) < o.val:
                        need[k] = o.val
            for k, v in need.items():
                eng.wait_ge(k, v)
                waited[k] = v
            ins = op.fn(eng)
            if op.signal:
                ins.then_inc(op.sem, 16 if op.is_dma else 1)

    def emit_all(self, tail_waits=()):
        nc = self.nc
        self.finalize()
        with nc.Block() as block:
            @block.tensor
            def _(eng):
                self.emit("pe", eng)

            @block.scalar
            def _(eng):
                self.emit("act", eng)

            @block.vector
            def _(eng):
                self.emit("dve", eng)

            @block.gpsimd
            def _(eng):
                self.emit("pool", eng)

            @block.sync
            def _(eng):
                self.emit("sp", eng)
                for (s, v) in tail_waits:
                    eng.wait_ge(s, v)


class Stream:
    def __init__(self, S, name, nslots, issue_fn):
        self.S = S
        self.name = name
        self.n = nslots
        self.issue_fn = issue_fn
        self.descs = []
        self.next_issue = 0
        self.next_get = 0
        self.sems = [S.dma_sem("%s_s%d" % (name, i)) for i in range(nslots)]

    def reset(self):
        self.next_issue = 0
        self.next_get = 0

    def _issue(self, i):
        slot = i % self.n
        for fn in self.issue_fn(slot, self.descs[i]):
            self.S.add("pool", fn, writes=[(self.name, slot)], dma_sem=self.sems[slot])

    def get(self, desc):
        i = self.next_get
        self.next_get += 1
        if self.S.dry:
            self.descs.append(desc)
            return i % self.n
        assert self.descs[i] == desc, (self.name, i, self.descs[i], desc)
        upto = min(len(self.descs), i + self.n - 1)
        while self.next_issue < upto:
            self._issue(self.next_issue)
            self.next_issue += 1
        return i % self.n


def build_nc(layers=(0, 1, 2, 3), final_norm=True, nseq=SEQ_PER_CORE):
    nc = bass.Bass("TRN2", target_bir_lowering=False)
    dt = lambda name, shape, kind="ExternalInput": nc.dram_tensor(name, list(shape), F32, kind=kind).ap()
    x_d = dt("x", [nseq, S_LEN, D])
    w_in_d = dt("w_in", [DEPTH, D, 3072])
    w_out_d = dt("w_out", [DEPTH, D, D])
    w_up_d = dt("w_up", [DEPTH, D, 2 * DFF])
    w_down_d = dt("w_down", [DEPTH, DFF, D])
    ta_d = dt("ta", [DEPTH, 8, 128, 640])
    tb_d = dt("tb", [8, 128, 640])
    g1_d = dt("g1", [128, DEPTH * 8])
    g2_d = dt("g2", [128, DEPTH * 8])
    gf_d = dt("gf", [128, 8])
    cw_d = dt("cw", [128, DEPTH * 3 * 44])
    cb_d = dt("cb", [128, DEPTH * 44])
    sg_d = dt("sg", [128, DEPTH])
    lam_d = dt("lam", [128, 4 * DEPTH * 64])
    t5f_d = dt("t5f", [128, 8])
    out_d = dt("out", [nseq, S_LEN, D], kind="ExternalOutput")

    with ExitStack() as st:
        S = Sched(nc, st)
        sb = lambda n, s, d: st.enter_context(nc.sbuf_tensor(n, list(s), d))
        xT = sb("xT", [128, 8, S_LEN], F32)
        hT = sb("hT", [128, 8, S_LEN], BF16)
        U = sb("U", [128, 22528], BF16)
        ct = sb("ct", [128, 2, 1024], F32)
        PT = sb("PT", [128, 8, 512], BF16)
        TAB = sb("TAB", [128, 4, 640], BF16)
        wr = sb("wr", [128, NS_W, 4096], BF16)
        sq = sb("sq", [128, 2, 512], BF16)
        lnv = sb("lnv", [128, 512], F32)
        rstd = sb("rstd", [128, 512], F32)
        scr = sb("scr", [128, 5, 512], F32)
        r0 = scr[:, 0, :]
        r1 = scr[:, 1, :]
        av = scr[:, 2, :]
        bv = scr[:, 3, :]
        ov = scr[:, 4, :]
        scrf = scr[:, :, :].rearrange("p a t -> p (a t)")
        lamr = scrf[:, 0:4 * DEPTH * 64]
        lamp = scrf[:, 4 * DEPTH * 64:6 * DEPTH * 64]
        carry = sb("carry", [128, 44, 2], F32)
        identb = sb("identb", [128, 128], BF16)
        identf = sb("identf", [128, 128], F32)
        onesb = sb("onesb", [128, 128], BF16)
        eps_t = sb("eps_t", [128, 1], F32)
        g1 = sb("g1s", [128, DEPTH * 8], F32)
        g2 = sb("g2s", [128, DEPTH * 8], F32)
        gf = sb("gfs", [128, 8], F32)
        cw = sb("cws", [128, DEPTH * 3 * 44], F32)
        cb = sb("cbs", [128, DEPTH * 44], F32)
        sg = sb("sgs", [128, DEPTH], F32)
        lams = sb("lams", [128, 2 * DEPTH], F32)
        nlam = sb("nlam", [128, DEPTH], F32)
        t5f = sb("t5fs", [128, 8], F32)
        ps = st.enter_context(nc.psum_tensor("ps", [128, 8, 512], F32))

        OT = U[:, 0:8192].rearrange("p (c t) -> p c t", c=4)
        Vb = U[:, 8192:16384].rearrange("p (i n) -> p i n", i=16)
        QT0 = U[:, 16384:18432]
        KT = U[:, 18432:20480]
        QT1 = U[:, 20480:22528]
        QTS = (QT0, QT1)
        Gb = U[:, 0:22528].rearrange("p (j t) -> p j t", j=NJ)

        def psr(b):
            return [("ps", b, 0), ("ps", b, 1)]

        def w_issue(slot, desc):
            fns = []
            kc, ntot, parts = desc[0], desc[1], desc[2]
            dst = wr[:, slot, 0:kc * ntot].rearrange("p (c n) -> p c n", c=kc)
            for (which, l, r0_, c0, n, off) in parts:
                src_t = {"in": w_in_d, "out": w_out_d, "up": w_up_d, "down": w_down_d}[which]
                src = src_t[l, r0_:r0_ + kc * 128, c0:c0 + n].rearrange("(c p) n -> p c n", p=128)
                fns.append((lambda d_, s_: (lambda e: e.dma_start(out=d_, in_=s_)))(dst[:, :, off:off + n], src))
            return fns

        def tab_issue(slot, desc):
            kind, l, h = desc
            src = ta_d[l, h] if kind == "a" else tb_d[h]
            return [(lambda d_, s_: (lambda e: e.dma_start(out=d_, in_=s_)))(TAB[:, slot, :], src)]

        WS = Stream(S, "w", NS_W, w_issue)
        TS = Stream(S, "tab", 4, tab_issue)

        def wtile(kc, ntot, parts):
            slot = WS.get((kc, ntot, tuple(parts)))
            view = wr[:, slot, 0:kc * ntot].rearrange("p (c n) -> p c n", c=kc)
            return slot, view

        def mm(out, lhsT, rhs, start, stop, reads, writes):
            S.add("pe", lambda e: e.matmul(out, lhsT, rhs, start=start, stop=stop), reads=reads, writes=writes)

        def tr(out, in_, reads, writes):
            S.add("pe", lambda e: e.transpose(out, in_, identf[:]), reads=reads, writes=writes)

        def act(out, in_, func, reads, writes, **kw):
            S.add("act", lambda e: e.activation(out, in_, func, **kw), reads=reads, writes=writes)

        def tt(out, a, b, op, reads, writes, eng="dve"):
            S.add(eng, lambda e: e.tensor_tensor(out, a, b, op), reads=reads, writes=writes)

        def stt(out, in0, scalar, in1, op0, op1, reads, writes, eng="dve"):
            S.add(eng, lambda e: e.scalar_tensor_tensor(out, in0, scalar, in1, op0, op1), reads=reads, writes=writes)

        def ts1(out, in0, scalar, op, reads, writes):
            S.add("dve", lambda e: e.tensor_scalar(out, in0, scalar, None, op), reads=reads, writes=writes)

        def cp(out, in_, reads, writes, eng="dve"):
            S.add(eng, lambda e: e.tensor_copy(out, in_), reads=reads, writes=writes)

        def recip(out, in_, reads, writes):
            S.add("dve", lambda e: e.reciprocal(out, in_), reads=reads, writes=writes)

        def dma(eng, out, in_, reads, writes, sem):
            return S.add(eng, lambda e: e.dma_start(out=out, in_=in_), reads=reads, writes=writes, dma_sem=sem)

        def prologue():
            dc = S.dma_sem("dconst")
            cops = []
            for (dst, src, nm) in [(g1[:], g1_d, "g1"), (g2[:], g2_d, "g2"), (gf[:], gf_d, "gf"), (cw[:], cw_d, "cw"),
                                   (cb[:], cb_d, "cb"), (sg[:], sg_d, "sg"), (lamr, lam_d, "lamr"), (t5f[:], t5f_d, "t5f")]:
                cops.append(dma("sp", dst, src, [], [nm], dc))
            for o in cops:
                o.val = S.dma_cum[dc]
            S.add("pool", lambda e: e.memset(identf[:], 0.0), writes=["identf"])
            S.add("pool", lambda e: e.affine_select(out=identf[:], in_=identf[:], pattern=[[-1, 128]],
                                                    compare_op=ALU.not_equal, fill=1.0, base=0,
                                                    channel_multiplier=1),
                  reads=["identf"], writes=["identf"])
            S.add("pool", lambda e: e.memset(onesb[:], 1.0), writes=["onesb"])
            S.add("pool", lambda e: e.memset(eps_t[:], RMS_EPS), writes=["eps"])
            S.add("pool", lambda e: e.memset(carry[:], 0.0), writes=["carry"])
            cp(identb[:], identf[:], ["identf"], ["identb"])
            n64 = DEPTH * 64
            tt(lamp[:, 0:n64], lamr[:, 0:n64], lamr[:, n64:2 * n64], ALU.mult, ["lamr"], ["lamp0"])
            tt(lamp[:, n64:2 * n64], lamr[:, 2 * n64:3 * n64], lamr[:, 3 * n64:4 * n64], ALU.mult, ["lamr"], ["lamp1"])
            for i in range(2 * DEPTH):
                S.add("dve", (lambda o_, i_: (lambda e: e.tensor_reduce(o_, i_, AX.X, ALU.add)))(
                    lams[:, i:i + 1], lamp[:, i * 64:(i + 1) * 64]),
                      reads=["lamp0", "lamp1"], writes=[("lams", i)])
            act(lams[:], lams[:], AF.Exp, [("lams", i) for i in range(2 * DEPTH)], ["lamse"])
            tt(nlam[:], lams[:, DEPTH:2 * DEPTH], lams[:, 0:DEPTH], ALU.subtract, ["lamse"], ["nlam0"])
            for l in range(DEPTH):
                li = 0.8 - 0.6 * math.exp(-0.3 * l)
                ts1(nlam[:, l:l + 1], nlam[:, l:l + 1], -li, ALU.add, ["nlam0"], [("nlam", l)])
                ts1(sg[:, l:l + 1], sg[:, l:l + 1], 1.0 - li, ALU.mult, ["sg"], [("sg", l)])

        def rmsnorm_to(gain_tile, gcol0, dst_t, dst_name, after_tile=None):
            for t4 in range(4):
                if after_tile is not None and t4 >= 2:
                    after_tile(t4 - 2)
                tsl = slice(t4 * 512, (t4 + 1) * 512)
                for c in range(8):
                    sl = c % 2
                    act(sq[:, sl, :], xT[:, c, tsl], AF.Square, [("xT", c, t4)], [("sq", sl)])
                    mm(ps[:, 7, :], onesb[:], sq[:, sl, :], c == 0, c == 7, [("sq", sl)], psr(7))
                act(lnv[:], ps[:, 7, :], AF.Ln, psr(7), [("lnv", 0), ("lnv", 1)], bias=eps_t[:, 0:1], scale=1.0 / D)
                act(rstd[:], lnv[:], AF.Exp, [("lnv", 0), ("lnv", 1)], ["rstd"], scale=-0.5)
                for c in range(8):
                    stt(dst_t[:, c, tsl], xT[:, c, tsl], gain_tile[:, gcol0 + c:gcol0 + c + 1], rstd[:],
                        ALU.mult, ALU.mult, [("xT", c, t4), "rstd"], [(dst_name, c, t4)])
            if after_tile is not None:
                after_tile(2)
                after_tile(3)

        def load_x(s):
            for i in range(16):
                sl = i % 2
                dma("sp", ct[:, sl, :], x_d[s, i * 128:(i + 1) * 128, :], [], [("ct", sl)], xsem[sl])
                for half in range(2):
                    b = 2 * (i % 2) + half
                    for c4 in range(4):
                        c = half * 4 + c4
                        tr(ps[:, b, c4 * 128:(c4 + 1) * 128], ct[:, sl, c * 128:(c + 1) * 128], [("ct", sl)], psr(b))
                    dst = xT[:, half * 4:half * 4 + 4, i * 128:(i + 1) * 128]
                    src = ps[:, b, :].rearrange("p (c t) -> p c t", c=4)
                    wr_ = [("xT", half * 4 + c4, i // 4) for c4 in range(4)]
                    if half == 0:
                        act(dst, src, AF.Copy, psr(b), wr_)
                    else:
                        cp(dst, src, psr(b), wr_)

        def store_out(s):
            for i in range(16):
                sl = i % 2
                for half in range(2):
                    b = 2 * (i % 2) + half
                    for c4 in range(4):
                        c = half * 4 + c4
                        tr(ps[:, b, c4 * 128:(c4 + 1) * 128], xT[:, c, i * 128:(i + 1) * 128], [("xT", c, i // 4)], psr(b))
                    dst = ct[:, sl, half * 512:(half + 1) * 512]
                    if half == 0:
                        act(dst, ps[:, b, :], AF.Copy, psr(b) + [("ct", sl)], [("ct", sl, half)])
                    else:
                        cp(dst, ps[:, b, :], psr(b) + [("ct", sl)], [("ct", sl, half)])
                dma("sp", out_d[s, i * 128:(i + 1) * 128, :], ct[:, sl, :],
                    [("ct", sl, 0), ("ct", sl, 1)], [("ct", sl)], osem)

        def proj_v(l, c0, chunks=None):
            slot, W = wtile(8, 512, [("in", l, 0, c0, 512, 0)])
            if chunks is not None:
                def chunk(t4):
                    for i in range(4 * t4, 4 * t4 + 4):
                        b = i % 3
                        for k in range(8):
                            mm(ps[:, b, :], hT[:, k, i * 128:(i + 1) * 128], W[:, k, :], k == 0, k == 7,
                               [("hT", k, i // 4), ("w", slot)], psr(b))
                        cp(Vb[:, i, :], ps[:, b, :], psr(b), [("V", i)])
                return chunk
            for i in range(16):
                b = i % 3
                for k in range(8):
                    mm(ps[:, b, :], hT[:, k, i * 128:(i + 1) * 128], W[:, k, :], k == 0, k == 7,
                       [("hT", k, i // 4), ("w", slot)], psr(b))
                cp(Vb[:, i, :], ps[:, b, :], psr(b), [("V", i)])

        def proj_qk(l, cq, ck):
            slot, W = wtile(8, 256, [("in", l, 0, cq, 128, 0), ("in", l, 0, ck, 128, 128)])
            for which in range(2):
                for t4 in range(4):
                    b = (which * 4 + t4) % 3
                    for k in range(8):
                        mm(ps[:, b, :], W[:, k, which * 128:(which + 1) * 128], hT[:, k, t4 * 512:(t4 + 1) * 512],
                           k == 0, k == 7, [("hT", k, t4), ("w", slot)], psr(b))
                    tsl = slice(t4 * 512, (t4 + 1) * 512)
                    if which == 0:
                        ts1(QT0[0:64, tsl], ps[0:64, b, :], 0.125, ALU.mult, psr(b), [("QT", 0, t4)])
                        ts1(QT1[64:128, tsl], ps[64:128, b, :], 0.125, ALU.mult, psr(b), [("QT", 1, t4)])
                    else:
                        cp(KT[:, tsl], ps[:, b, :], psr(b), [("KT", t4)])

        st_rot = [0]
        pt_rot = [0]
        ST_BANKS = (0, 1, 2, 7)
        PIPE_DEPTH = 3

        def next_st():
            b = ST_BANKS[st_rot[0] % 4]
            st_rot[0] += 1
            return b

        carry_deferred = []

        def flush_deferred():
            while carry_deferred:
                carry_deferred.pop(0)[1]()

        def attn_steps_run(steps):
            n = len(steps)
            deferred = carry_deferred
            for i in range(n + PIPE_DEPTH):
                S.begin_group("pe")
                if i < n:
                    steps[i][0]()
                if i >= PIPE_DEPTH:
                    steps[i - PIPE_DEPTH][1]()
                S.end_group("pe")
                for dq in deferred:
                    dq[0] -= 1
                while deferred and deferred[0][0] <= 0:
                    deferred.pop(0)[1]()
                if i >= PIPE_DEPTH and steps[i - PIPE_DEPTH][2] is not None:
                    later = steps[i - PIPE_DEPTH][2]()
                    if later is not None:
                        deferred.append([8, later])

        def score_step(kt_ap, qt_ap, c0, c1, tab_ap, tab_res, exp_bias, qk_reads, pv_list, post=None):
            def qk():
                b = next_st()
                pslot = pt_rot[0] % 8
                pt_rot[0] += 1
                st_[0] = (b, pslot)
                mm(ps[:, b, c0:c1], kt_ap, qt_ap, True, tab_ap is None, qk_reads, psr(b))
                if tab_ap is not None:
                    mm(ps[:, b, c0:c1], identb[:], tab_ap, False, True, [tab_res], psr(b))
                if exp_bias is None:
                    act(PT[:, pslot, c0:c1], ps[:, b, c0:c1], AF.Exp, psr(b), [("PT", pslot)])
                else:
                    act(PT[:, pslot, c0:c1], ps[:, b, c0:c1], AF.Exp, psr(b), [("PT", pslot)], bias=exp_bias)

            st_ = [None]

            def pv():
                b, pslot = st_[0]
                for (out_ap, lhsT_ap, start, stop, reads, writes) in pv_list:
                    mm(out_ap, lhsT_ap, PT[:, pslot, c0:c1], start, stop, [("PT", pslot)] + reads, writes)

            return (qk, pv, post)

        def recip_act(out, in_ps, in_res, out_res):
            act(lnv[:], in_ps, AF.Ln, in_res, ["lnv"])
            act(out, lnv[:], AF.Exp, ["lnv"], [out_res], scale=-1.0)

        def zero_q_pads():
            S.add("dve", lambda e: e.memset(QT0[64:128, :], 0.0), writes=[("G", 16), ("G", 17), ("QTz", 0)])
            S.add("dve", lambda e: e.memset(QT1[0:64, :], 0.0), writes=[("G", 20), ("G", 21), ("QTz", 1)])

        def attn_core(l, kind, unit, tslots, post_fns):
            steps = []
            for qt in range(4):
                for hh in range(2):
                    bO, bS = 3 + 2 * hh, 4 + 2 * hh
                    if kind == "a":
                        rs = [r for r in (4, 3, 5, 2, 6, 1, 7, 0) if 4 * qt - 4 + r >= 0]
                        blocks = []
                        for r in rs:
                            kb = 4 * qt - 4 + r
                            lo, hi = max(0, 2 * r - 8), min(7, 2 * r + 1)
                            c0, c1 = 64 * lo, 64 * (hi + 1)
                            v0 = 512 - 128 * r
                            blocks.append((kb, c0, c1, TAB[:, tslots[hh], v0 + c0:v0 + c1], ("tab", tslots[hh]), None))
                        vcol = slice(unit * 128, (unit + 1) * 128)
                    else:
                        m = 2 * unit + hh
                        blocks = []
                        kbs = list(range(4 * qt, 4 * qt + 4)) + ([4 * qt - 1] if qt > 0 else []) + list(range(0, max(0, 4 * qt - 1)))
                        for kb in kbs:
                            j = 4 * qt - kb
                            if j >= 2:
                                blocks.append((kb, 0, 512, None, None, t5f[:, m:m + 1]))
                            elif j == 1:
                                blocks.append((kb, 0, 512, TAB[:, tslots[hh], 128:640], ("tab", tslots[hh]), None))
                            else:
                                c0 = 128 * (-j)
                                blocks.append((kb, c0, 512, TAB[:, tslots[hh], 0:512 - c0], ("tab", tslots[hh]), None))
                        vcol = slice(unit * 128, (unit + 1) * 128)
                    nb = len(blocks)
                    for idx, (kb, c0, c1, tab_ap, tab_res, ebias) in enumerate(blocks):
                        first, last = idx == 0, idx == nb - 1
                        pv_list = [
                            (ps[:, bO, c0:c1], Vb[:, kb, vcol], first, last, [("V", kb)], psr(bO)),
                            (ps[:, bS, c0:c1], onesb[:], first, last, [], psr(bS)),
                        ]
                        post = None
                        if last:
                            post = (lambda f=post_fns[hh], qt_=qt: f(qt_))
                        steps.append(score_step(
                            KT[:, kb * 128:(kb + 1) * 128],
                            QTS[hh][:, qt * 512 + c0:qt * 512 + c1],
                            c0, c1, tab_ap, tab_res, ebias, [("KT", kb // 4), ("QT", hh, qt), ("QTz", hh)], pv_list,
                            post=post))
            attn_steps_run(steps)

        def attn_A(l, v_done=False):
            if not v_done:
                proj_v(l, 1024)
            for p in range(4):
                proj_qk(l, 128 * p, 512 + 128 * p)
                tslots = [TS.get(("a", l, 2 * p + hh)) for hh in range(2)]

                def mk_post(hh, p=p):
                    lo_, hi_ = 64 * hh, 64 * hh + 64
                    bO, bS = 3 + 2 * hh, 4 + 2 * hh
                    rr = r0 if hh == 0 else r1

                    def post(qt):
                        tsl = slice(qt * 512, (qt + 1) * 512)
                        act(lnv[lo_:hi_, :], ps[lo_:hi_, bS, :], AF.Ln, psr(bS), [("lnv", hh)])
                        act(rr[lo_:hi_, :], lnv[lo_:hi_, :], AF.Exp, [("lnv", hh)], [("rr", hh)], scale=-1.0)
                        tt(OT[lo_:hi_, p, tsl], ps[lo_:hi_, bO, :], rr[lo_:hi_, :], ALU.mult,
                           psr(bO) + [("rr", hh)], [("OT", p, qt, hh)])
                    return post

                attn_core(l, "a", p, tslots, [mk_post(0), mk_post(1)])
            flush_deferred()
            w_out_round(l, 0)

        def attn_B(l):
            proj_v(l, 2560)
            for hb in range(4):
                proj_qk(l, 1536 + 128 * hb, 2048 + 128 * hb)
                tslots = [TS.get(("b", 0, 2 * hb + mmi)) for mmi in range(2)]

                def post0(qt):
                    act(lnv[:], ps[:, 4, :], AF.Ln, psr(4), [("lnv", 0), ("lnv", 1)])
                    act(r0, lnv[:], AF.Exp, [("lnv", 0), ("lnv", 1)], [("rr", 0)], scale=-1.0)
                    tt(av, ps[:, 3, :], r0, ALU.mult, psr(3) + [("rr", 0)], ["av"])

                def post1(qt, hb=hb):
                    tsl = slice(qt * 512, (qt + 1) * 512)
                    act(lnv[:], ps[:, 6, :], AF.Ln, psr(6), [("lnv", 0), ("lnv", 1)])
                    act(r1, lnv[:], AF.Exp, [("lnv", 0), ("lnv", 1)], [("rr", 1)], scale=-1.0)
                    stt(bv, ps[:, 5, :], nlam[:, l:l + 1], r1, ALU.mult, ALU.mult, psr(5) + [("rr", 1)], ["bv"])
                    tt(ov, av, bv, ALU.add, ["av", "bv"], ["ov"])
                    tt(sq[:, 0, :], ov, ov, ALU.mult, ["ov"], [("sq", 0)])

                    def later():
                        b7 = next_st()
                        mm(ps[:, b7, :], onesb[:], sq[:, 0, :], True, True, [("sq", 0)], psr(b7))
                        act(lnv[:], ps[:, b7, :], AF.Ln, psr(b7), [("lnv", 0), ("lnv", 1)], bias=eps_t[:, 0:1], scale=1.0 / 128)
                        act(rstd[:], lnv[:], AF.Exp, [("lnv", 0), ("lnv", 1)], ["rstd"], scale=-0.5)
                        stt(OT[:, hb, tsl], ov, sg[:, l:l + 1], rstd[:], ALU.mult, ALU.mult,
                            ["ov", "rstd"], [("OT", hb, qt, 0), ("OT", hb, qt, 1)])
                    return later

                attn_core(l, "b", hb, tslots, [post0, post1])
            flush_deferred()
            w_out_round(l, 1)

        def w_out_round(l, rnd):
            for ocg in range(2):
                slot, W = wtile(4, 512, [("out", l, rnd * 512, ocg * 512, 512, 0)])
                for oc4 in range(4):
                    oc = ocg * 4 + oc4
                    for t4 in range(4):
                        b = (oc4 * 4 + t4) % 3
                        tsl = slice(t4 * 512, (t4 + 1) * 512)
                        for ic in range(4):
                            mm(ps[:, b, :], W[:, ic, oc4 * 128:(oc4 + 1) * 128], OT[:, ic, tsl], ic == 0, ic == 3,
                               [("OT", ic, t4, 0), ("OT", ic, t4, 1), ("w", slot)], psr(b))
                        tt(xT[:, oc, tsl], xT[:, oc, tsl], ps[:, b, :], ALU.add, psr(b) + [("xT", oc, t4)], [("xT", oc, t4)])

        def ffn(l):
            cwb = l * 3 * 44
            for half in range(2):
                jn = 0
                for jg in range(11):
                    nj = 2
                    slot, W = wtile(8, 512, [("up", l, 0, 256 * jg, 256, 0), ("up", l, 0, DFF + 256 * jg, 256, 256)])
                    for jj in range(nj):
                        j = jg * 2 + jj
                        for gv in range(2):
                            b0 = 4 * (jn % 2) + 2 * gv
                            ch = j + NJ * gv
                            wc = 256 * gv + jj * 128
                            for t2 in range(2):
                                t4 = 2 * half + t2
                                for k in range(8):
                                    mm(ps[:, b0 + t2, :], W[:, k, wc:wc + 128], hT[:, k, t4 * 512:(t4 + 1) * 512],
                                       k == 0, k == 7, [("hT", k, t4), ("w", slot)], psr(b0 + t2))
                            pu = ps[:, b0:b0 + 2, :].rearrange("p a t -> p (a t)")
                            a = ct[:, gv, :]
                            w0 = cw[:, cwb + ch:cwb + ch + 1]
                            w1 = cw[:, cwb + 44 + ch:cwb + 44 + ch + 1]
                            w2 = cw[:, cwb + 88 + ch:cwb + 88 + ch + 1]
                            bb = cb[:, l * 44 + ch:l * 44 + ch + 1]
                            pres = psr(b0) + psr(b0 + 1)
                            act(a, pu, AF.Identity, pres, [("ct", gv)], bias=bb, scale=w2)
                            stt(a[:, 1:1024], pu[:, 0:1023], w1, a[:, 1:1024], ALU.mult, ALU.add, pres + [("ct", gv)], [("ct", gv)])
                            stt(a[:, 2:1024], pu[:, 0:1022], w0, a[:, 2:1024], ALU.mult, ALU.add, pres + [("ct", gv)], [("ct", gv)])
                            if half == 0:
                                act(carry[:, ch, :], pu[:, 1022:1024], AF.Copy, pres, [("carry", ch)])
                            else:
                                stt(a[:, 0:2], carry[:, ch, :], w0, a[:, 0:2], ALU.mult, ALU.add,
                                    [("carry", ch), ("ct", gv)], [("ct", gv)])
                                stt(a[:, 0:1], carry[:, ch, 1:2], w1, a[:, 0:1], ALU.mult, ALU.add,
                                    [("carry", ch), ("ct", gv)], [("ct", gv)])
                        act(ct[:, 0, :], ct[:, 0, :], AF.Silu, [("ct", 0)], [("ct", 0)])
                        tt(Gb[:, j, :], ct[:, 0, :], ct[:, 1, :], ALU.mult, [("ct", 0), ("ct", 1)], [("G", j)])
                        jn += 1
                for ocg in range(2):
                    for jg3 in range(3):
                        j0 = 8 * jg3
                        njj = 8 if jg3 < 2 else 6
                        slot, W = wtile(njj, 512, [("down", l, j0 * 128, ocg * 512, 512, 0)])
                        for oc4 in range(4):
                            for t2 in range(2):
                                b = oc4 * 2 + t2
                                for jj in range(njj):
                                    j = j0 + jj
                                    mm(ps[:, b, :], W[:, jj, oc4 * 128:(oc4 + 1) * 128], Gb[:, j, t2 * 512:(t2 + 1) * 512],
                                       j == 0, j == NJ - 1, [("G", j), ("w", slot)], psr(b))
                    for oc4 in range(4):
                        oc = ocg * 4 + oc4
                        for t2 in range(2):
                            b = oc4 * 2 + t2
                            t4 = 2 * half + t2
                            tsl = slice(t4 * 512, (t4 + 1) * 512)
                            tt(xT[:, oc, tsl], xT[:, oc, tsl], ps[:, b, :], ALU.add, psr(b) + [("xT", oc, t4)], [("xT", oc, t4)])

        def body():
            for s in range(nseq):
                load_x(s)
                for l in layers:
                    vchunk = proj_v(l, 1024, chunks=True)
                    rmsnorm_to(g1, l * 8, hT, "hT", after_tile=vchunk)
                    zero_q_pads()
                    attn_A(l, v_done=True)
                    attn_B(l)
                    rmsnorm_to(g2, l * 8, hT, "hT")
                    ffn(l)
                if final_norm:
                    rmsnorm_to(gf, 0, xT, "xT")
                store_out(s)

        xsem = [S.dma_sem("xs0"), S.dma_sem("xs1")]
        osem = S.dma_sem("osem")
        S.dry = True
        body()
        S.dry = False
        WS.reset()
        TS.reset()
        st_rot[0] = 0
        pt_rot[0] = 0
        prologue()
        S.const_reads = ["g1", "g2", "gf", "cw", "cb", "t5f", "identf", "identb", "onesb", "eps", "carry"] + \
            [("nlam", l) for l in range(DEPTH)] + [("sg", l) for l in range(DEPTH)]
        body()
        S.emit_all(tail_waits=[(osem, S.dma_cum[osem])])
    return nc


def _t5_bucket_np(rel):
    rel = np.asarray(rel, np.int32)
    nb = 16
    ret = np.where(rel > 0, nb, 0)
    n = np.abs(rel)
    max_exact = 8
    is_small = n < max_exact
    nf = np.maximum(n, 1).astype(np.float32)
    large = max_exact + (np.log(nf / np.float32(max_exact)) / np.float32(math.log(128 / max_exact))
                         * np.float32(nb - max_exact)).astype(np.int32)
    large = np.minimum(large, nb - 1)
    return ret + np.where(is_small, n, large)


def _host_prep(inp):
    f32 = np.float32
    k = np.arange(128)[:, None]
    v = np.arange(640)[None, :]
    dist = v - k
    dch = v // 64 - k // 64
    idx = np.clip(dist, -128, 128) + 128
    valid_a = (dch >= 0) & (dch <= 8)
    arb = np.asarray(inp["a_rel_bias"], f32)
    ta = arb[:, :, idx]
    ta = np.where(valid_a[None, None], ta, f32(NEG)).astype(f32)
    bucket = _t5_bucket_np(-dist)
    t5 = np.asarray(inp["t5_bias"], f32)
    tb = np.transpose(t5[bucket], (2, 0, 1))
    valid_b = (k // 64) <= (v // 64)
    tb = np.where(valid_b[None], tb, f32(NEG)).astype(f32)
    t5f = np.ascontiguousarray(np.broadcast_to(t5[15][None, :], (128, 8))).astype(f32)

    def fm(a, nchunk):
        a = np.asarray(a, f32)
        L = a.shape[0]
        return np.ascontiguousarray(a.reshape(L, nchunk, 128).transpose(2, 0, 1).reshape(128, L * nchunk))

    g1 = fm(inp["attn_norm_g"], 8)
    g2 = fm(inp["ffn_norm_g"], 8)
    gf = fm(np.asarray(inp["final_norm_g"], f32)[None], 8)
    cwh = np.asarray(inp["conv_w"], f32).reshape(DEPTH * 3, 44 * 128)
    cw = fm(cwh, 44)
    cb = fm(inp["conv_b"], 44)
    sg = np.ascontiguousarray(np.asarray(inp["subln_g"], f32).T)
    lam = np.concatenate([np.asarray(inp[n], f32).reshape(-1) for n in
                          ("lambda_q1", "lambda_k1", "lambda_q2", "lambda_k2")])
    lam = np.ascontiguousarray(np.broadcast_to(lam[None, :], (128, lam.size))).astype(f32)
    return dict(ta=ta, tb=tb, t5f=t5f, g1=g1, g2=g2, gf=gf, cw=cw, cb=cb, sg=sg, lam=lam)


_NC_CACHE = {}


def kernel(x, attn_norm_g, w_in, a_rel_bias, t5_bias, lambda_q1, lambda_k1, lambda_q2, lambda_k2,
           subln_g, w_out, ffn_norm_g, w_up, conv_w, conv_b, w_down, final_norm_g):
    inp = dict(x=x, attn_norm_g=attn_norm_g, w_in=w_in, a_rel_bias=a_rel_bias, t5_bias=t5_bias,
               lambda_q1=lambda_q1, lambda_k1=lambda_k1, lambda_q2=lambda_q2, lambda_k2=lambda_k2,
               subln_g=subln_g, w_out=w_out, ffn_norm_g=ffn_norm_g, w_up=w_up, conv_w=conv_w,
               conv_b=conv_b, w_down=w_down, final_norm_g=final_norm_g)
    hp = _host_prep(inp)
    shared = dict(
        w_in=np.ascontiguousarray(np.asarray(w_in, np.float32)),
        w_out=np.ascontiguousarray(np.asarray(w_out, np.float32)),
        w_up=np.ascontiguousarray(np.asarray(w_up, np.float32)),
        w_down=np.ascontiguousarray(np.asarray(w_down, np.float32)),
        **hp)
    xs = np.asarray(x, np.float32)
    nc = build_nc()
    in_maps = []
    for c in range(N_CORES):
        m = dict(shared)
        m["x"] = np.ascontiguousarray(xs[c * SEQ_PER_CORE:(c + 1) * SEQ_PER_CORE])
        in_maps.append(m)
    res = run_bass_kernel_spmd(nc, in_maps, core_ids=list(range(N_CORES)))
    out = np.concatenate([np.asarray(r["out"], np.float32) for r in res.results], axis=0)
    return out
```
